# Optimizing a Trainium2 kernel written in Bass

```python
import jax, jax.numpy as jnp
from jax import lax
import numpy as np

D_MODEL = 1024
BATCH = 16
SEQ = 4096
DEPTH = 2
DEC_BATCH = 32
DEC_SEQ = 64
PAST_LEN = 4096

CHUNK = 64
N_MIXERS = 2
N_POOL_LAYERS = (DEPTH + 1) // 2
N_GDN_LAYERS = DEPTH // 2
EPS = 1e-6

POOL_EXPAND = 2
POOL_WIDTH = POOL_EXPAND * D_MODEL
POOL_WINDOWS = (2, 4, 8, 16)
N_POOL_GROUPS = len(POOL_WINDOWS)
POOL_GROUP_DIM = POOL_WIDTH // N_POOL_GROUPS
POOL_BUF = max(POOL_WINDOWS) - 1

GDN_HEAD_DIM = 128
GDN_QK_HEADS = D_MODEL // GDN_HEAD_DIM
GDN_V_HEADS = 2 * GDN_QK_HEADS
GDN_KEY_DIM = GDN_QK_HEADS * GDN_HEAD_DIM
GDN_VALUE_DIM = GDN_V_HEADS * GDN_HEAD_DIM
GDN_CONV_DIM = 2 * GDN_KEY_DIM + GDN_VALUE_DIM
GDN_CONV_WIDTH = 4
GDN_IN_DIM = GDN_CONV_DIM + GDN_VALUE_DIM + 2 * GDN_V_HEADS

kernel_name = 'hybrid_pool_gdn_streaming_step'


def _rmsnorm(x, gain):
    xf = x.astype(jnp.float32)
    r = lax.rsqrt(jnp.mean(xf * xf, axis=-1, keepdims=True) + EPS)
    return xf * r * gain.astype(jnp.float32)


def _l2norm(x):
    xf = x.astype(jnp.float32)
    return xf * lax.rsqrt(jnp.sum(xf * xf, axis=-1, keepdims=True) + EPS)


def _pool_mixer(h, buf, start, w_in, w_group, scale, w_out):
    bsz, seqlen, _ = h.shape
    proj = h @ w_in
    u, z = proj[..., :POOL_WIDTH], proj[..., POOL_WIDTH:]
    ext = jnp.concatenate([buf.astype(u.dtype), u], axis=1)
    cs = jnp.cumsum(ext.astype(jnp.float32), axis=1)
    cs = jnp.pad(cs, ((0, 0), (1, 0), (0, 0)))
    pos = (start + jnp.arange(seqlen)).astype(jnp.float32)
    p = POOL_BUF
    pooled = []
    for gi, w in enumerate(POOL_WINDOWS):
        lo, hi = gi * POOL_GROUP_DIM, (gi + 1) * POOL_GROUP_DIM
        wsum = cs[:, p + 1:p + 1 + seqlen, lo:hi] - cs[:, p + 1 - w:p + 1 - w + seqlen, lo:hi]
        cnt = jnp.minimum(pos + 1.0, float(w))
        pooled.append(wsum / cnt[None, :, None])
    pooled = jnp.concatenate(pooled, axis=-1)
    d = (pooled - u.astype(jnp.float32)).reshape(bsz, seqlen, N_POOL_GROUPS, POOL_GROUP_DIM)
    y = jnp.einsum('blgc,gcd->blgd', d, w_group.astype(jnp.float32)).reshape(bsz, seqlen, POOL_WIDTH)
    y = y * scale.astype(jnp.float32) * jax.nn.silu(z.astype(jnp.float32))
    out = y.astype(h.dtype) @ w_out
    return out, ext[:, -POOL_BUF:]


def _chunked_gated_delta(q, k, v, g, beta, s0):
    bsz, seqlen, nh, dk = q.shape
    dv = v.shape[-1]
    c = min(CHUNK, seqlen)
    n = seqlen // c
    f32 = jnp.float32

    def blk(t):
        return t.astype(f32).reshape(bsz, n, c, nh, t.shape[-1]).transpose(0, 3, 1, 2, 4)

    qb, kb_, vb = blk(q), blk(k), blk(v)
    gb = g.astype(f32).reshape(bsz, n, c, nh).transpose(0, 3, 1, 2)
    bb = beta.astype(f32).reshape(bsz, n, c, nh).transpose(0, 3, 1, 2)
    gc = jnp.cumsum(gb, axis=-1)
    idx = jnp.arange(c)
    incl = idx[:, None] >= idx[None, :]
    strict = idx[:, None] > idx[None, :]
    diff = gc[..., :, None] - gc[..., None, :]
    decay = jnp.where(incl, jnp.exp(jnp.where(incl, diff, 0.0)), 0.0)
    kbeta = kb_ * bb[..., None]
    lmat = jnp.where(strict, jnp.einsum('bhncd,bhnjd->bhncj', kbeta, kb_) * decay, 0.0)
    rhs = jnp.concatenate([vb * bb[..., None], kbeta * jnp.exp(gc)[..., None]], axis=-1)
    sol = lax.linalg.triangular_solve(lmat, rhs, left_side=True, lower=True, unit_diagonal=True)
    value, kcd = sol[..., :dv], sol[..., dv:]
    aqk = jnp.einsum('bhncd,bhnjd->bhncj', qb, kb_) * decay
    qdec = qb * jnp.exp(gc)[..., None]
    glast = gc[..., -1]
    kdec = kb_ * jnp.exp(glast[..., None] - gc)[..., None]
    xs = tuple(jnp.moveaxis(t, 2, 0) for t in (value, kcd, aqk, qdec, kdec, glast))

    def step(s, inp):
        val, kc, a, qd, kd, gl = inp
        v_new = val - jnp.einsum('bhcd,bhde->bhce', kc, s)
        o = jnp.einsum('bhcd,bhde->bhce', qd, s) + jnp.einsum('bhcj,bhje->bhce', a, v_new)
        s = s * jnp.exp(gl)[..., None, None] + jnp.einsum('bhcd,bhce->bhde', kd, v_new)
        return s, o

    s_fin, o = lax.scan(step, s0.astype(f32), xs)
    o = o.transpose(1, 0, 3, 2, 4).reshape(bsz, seqlen, nh, dv)
    return o, s_fin


def _gdn_mixer(h, conv_buf, s0, w_in, conv_w, a_log, dt_bias, norm_w, w_out):
    bsz, seqlen, _ = h.shape
    proj = h @ w_in
    o1 = GDN_CONV_DIM
    o2 = o1 + GDN_VALUE_DIM
    o3 = o2 + GDN_V_HEADS
    qkv, z, a, b = proj[..., :o1], proj[..., o1:o2], proj[..., o2:o3], proj[..., o3:]
    ext = jnp.concatenate([conv_buf.astype(qkv.dtype), qkv], axis=1)
    conv = ext[:, 0:seqlen] * conv_w[0]
    for t in range(1, GDN_CONV_WIDTH):
        conv = conv + ext[:, t:t + seqlen] * conv_w[t]
    conv = jax.nn.silu(conv)
    q = conv[..., :GDN_KEY_DIM].reshape(bsz, seqlen, GDN_QK_HEADS, GDN_HEAD_DIM)
    k = conv[..., GDN_KEY_DIM:2 * GDN_KEY_DIM].reshape(bsz, seqlen, GDN_QK_HEADS, GDN_HEAD_DIM)
    v = conv[..., 2 * GDN_KEY_DIM:].reshape(bsz, seqlen, GDN_V_HEADS, GDN_HEAD_DIM)
    rep = GDN_V_HEADS // GDN_QK_HEADS
    q = jnp.repeat(_l2norm(q) * GDN_HEAD_DIM ** -0.5, rep, axis=2)
    k = jnp.repeat(_l2norm(k), rep, axis=2)
    beta = jax.nn.sigmoid(b.astype(jnp.float32))
    g = -jnp.exp(a_log.astype(jnp.float32)) * jax.nn.softplus(a.astype(jnp.float32) + dt_bias.astype(jnp.float32))
    o, s_new = _chunked_gated_delta(q, k, v, g, beta, s0)
    zg = jax.nn.silu(z.astype(jnp.float32)).reshape(bsz, seqlen, GDN_V_HEADS, GDN_HEAD_DIM)
    o = _rmsnorm(o, norm_w) * zg
    out = o.reshape(bsz, seqlen, GDN_VALUE_DIM).astype(h.dtype) @ w_out
    return out, ext[:, -(GDN_CONV_WIDTH - 1):], s_new


def _trunk(x, c, pool_bufs, conv_bufs, rec_states, start,
           norm_gain, ada_w, ada_b, pool_w_in, pool_w_group, pool_scale, pool_w_out,
           gdn_w_in, gdn_conv_w, gdn_a_log, gdn_dt_bias, gdn_norm_w, gdn_w_out, final_gain):
    new_pool, new_conv, new_rec = [], [], []
    for i in range(DEPTH):
        mod = jax.nn.silu(c) @ ada_w[i] + ada_b[i]
        shift, scale, gate = jnp.split(mod, 3, axis=-1)
        h = (_rmsnorm(x, norm_gain[i]) * (1.0 + scale[:, None, :]) + shift[:, None, :]).astype(x.dtype)
        j = i // N_MIXERS
        if i % N_MIXERS == 0:
            out, buf = _pool_mixer(h, pool_bufs[j], start, pool_w_in[j], pool_w_group[j],
                                   pool_scale[j], pool_w_out[j])
            new_pool.append(buf)
        else:
            out, cbuf, s_new = _gdn_mixer(h, conv_bufs[j], rec_states[j], gdn_w_in[j], gdn_conv_w[j],
                                          gdn_a_log[j], gdn_dt_bias[j], gdn_norm_w[j], gdn_w_out[j])
            new_conv.append(cbuf)
            new_rec.append(s_new)
        x = x + gate[:, None, :] * out
    y = _rmsnorm(x, final_gain).astype(x.dtype)
    return y, jnp.stack(new_pool), jnp.stack(new_conv), jnp.stack(new_rec)


def setup_inputs(seed: int = 0) -> dict:
    key = jax.random.key(seed)
    ks = jax.random.split(key, 24)
    f32 = jnp.float32

    def nrm(k, shape, s):
        return jax.random.normal(k, shape, f32) * s

    dt = jnp.exp(jax.random.uniform(ks[17], (N_GDN_LAYERS, GDN_V_HEADS), f32, np.log(0.001), np.log(0.1)))
    return {
        'x_prompt': nrm(ks[0], (BATCH, SEQ, D_MODEL), 1.0),
        'x_sample': nrm(ks[1], (DEC_BATCH, DEC_SEQ, D_MODEL), 1.0),
        'c_prompt': nrm(ks[2], (BATCH, D_MODEL), 1.0),
        'c_sample': nrm(ks[3], (DEC_BATCH, D_MODEL), 1.0),
        'state_pool': nrm(ks[4], (N_POOL_LAYERS, DEC_BATCH, POOL_BUF, POOL_WIDTH), 1.0),
        'state_conv': nrm(ks[5], (N_GDN_LAYERS, DEC_BATCH, GDN_CONV_WIDTH - 1, GDN_CONV_DIM), 1.0),
        'state_rec': nrm(ks[6], (N_GDN_LAYERS, DEC_BATCH, GDN_V_HEADS, GDN_HEAD_DIM, GDN_HEAD_DIM), GDN_HEAD_DIM ** -0.5),
        'norm_gain': 1.0 + nrm(ks[7], (DEPTH, D_MODEL), 0.05),
        'ada_w': nrm(ks[8], (DEPTH, D_MODEL, 3 * D_MODEL), 0.5 * D_MODEL ** -0.5),
        'ada_b': nrm(ks[9], (DEPTH, 3 * D_MODEL), 0.02),
        'pool_w_in': nrm(ks[10], (N_POOL_LAYERS, D_MODEL, 2 * POOL_WIDTH), D_MODEL ** -0.5),
        'pool_w_group': nrm(ks[11], (N_POOL_LAYERS, N_POOL_GROUPS, POOL_GROUP_DIM, POOL_GROUP_DIM), POOL_GROUP_DIM ** -0.5),
        'pool_scale': 1.0 + nrm(ks[12], (N_POOL_LAYERS, POOL_WIDTH), 0.1),
        'pool_w_out': nrm(ks[13], (N_POOL_LAYERS, POOL_WIDTH, D_MODEL), POOL_WIDTH ** -0.5),
        'gdn_w_in': nrm(ks[14], (N_GDN_LAYERS, D_MODEL, GDN_IN_DIM), D_MODEL ** -0.5),
        'gdn_conv_w': nrm(ks[15], (N_GDN_LAYERS, GDN_CONV_WIDTH, GDN_CONV_DIM), 0.5),
        'gdn_a_log': jnp.log(jax.random.uniform(ks[16], (N_GDN_LAYERS, GDN_V_HEADS), f32, 1.0, 16.0)),
        'gdn_dt_bias': jnp.log(jnp.expm1(dt)),
        'gdn_norm_w': 1.0 + nrm(ks[18], (N_GDN_LAYERS, GDN_HEAD_DIM), 0.05),
        'gdn_w_out': nrm(ks[19], (N_GDN_LAYERS, GDN_VALUE_DIM, D_MODEL), GDN_VALUE_DIM ** -0.5),
        'final_gain': 1.0 + nrm(ks[20], (D_MODEL,), 0.05),
    }


def reference(x_prompt, x_sample, c_prompt, c_sample, state_pool, state_conv, state_rec,
              norm_gain, ada_w, ada_b, pool_w_in, pool_w_group, pool_scale, pool_w_out,
              gdn_w_in, gdn_conv_w, gdn_a_log, gdn_dt_bias, gdn_norm_w, gdn_w_out, final_gain):
    nb = x_prompt.shape[0]
    zero_pool = jnp.zeros((N_POOL_LAYERS, nb, POOL_BUF, POOL_WIDTH), x_prompt.dtype)
    zero_conv = jnp.zeros((N_GDN_LAYERS, nb, GDN_CONV_WIDTH - 1, GDN_CONV_DIM), x_prompt.dtype)
    zero_rec = jnp.zeros((N_GDN_LAYERS, nb, GDN_V_HEADS, GDN_HEAD_DIM, GDN_HEAD_DIM), jnp.float32)
    y_prompt, pool_p, conv_p, rec_p = _trunk(
        x_prompt, c_prompt, zero_pool, zero_conv, zero_rec, 0,
        norm_gain, ada_w, ada_b, pool_w_in, pool_w_group, pool_scale, pool_w_out,
        gdn_w_in, gdn_conv_w, gdn_a_log, gdn_dt_bias, gdn_norm_w, gdn_w_out, final_gain)
    y_sample, pool_s, conv_s, rec_s = _trunk(
        x_sample, c_sample, state_pool, state_conv, state_rec, PAST_LEN,
        norm_gain, ada_w, ada_b, pool_w_in, pool_w_group, pool_scale, pool_w_out,
        gdn_w_in, gdn_conv_w, gdn_a_log, gdn_dt_bias, gdn_norm_w, gdn_w_out, final_gain)
    return (y_prompt, y_sample, pool_p, conv_p, rec_p, pool_s, conv_s, rec_s)
```

```python
import numpy as np
from contextlib import ExitStack
import concourse.bass as bass
import concourse.mybir as mybir
from concourse.bass_utils import run_bass_kernel_spmd

F32 = mybir.dt.float32
BF16 = mybir.dt.bfloat16
AF = mybir.ActivationFunctionType
ALU = mybir.AluOpType
EPS = 1e-6
NCORES = 8
D = 1024
SEQ = 4096
TP = 256
NEGBIG = -30000.0
SAME_ENGINE_SYNC = True

PW_IN, WG, PW_OUT, GW_IN, GW_OUT = 0, 16, 20, 28, 52
NBLK = 60
V_GAIN, V_ADAB, V_PSCALE, V_CONVW, V_NORMW, V_FGAIN, NV = 0, 16, 64, 80, 208, 209, 217


class Tk:
    __slots__ = ("name", "w", "r", "x")

    def __init__(self, name, x=False):
        self.name = name
        self.w = None
        self.r = {}
        self.x = x


class Sched:
    def __init__(self, nc, es):
        self.nc = nc
        self.es = es
        self.eng = {"pe": nc.tensor, "act": nc.scalar, "dve": nc.vector, "pool": nc.gpsimd, "sp": nc.sync}
        self.sem = {}
        self.cnt = {}
        self.waited = {}
        self.dma_keys = []
        for k in self.eng:
            self.sem[k] = es.enter_context(nc.semaphore("s_" + k))
            self.cnt[k] = 0

    def dma_sem(self, key):
        if key not in self.sem:
            self.sem[key] = self.es.enter_context(self.nc.semaphore("d_" + key))
            self.cnt[key] = 0
            self.dma_keys.append(key)
        return key

    def _deps(self, R, W, eng=None):
        deps = {}

        def add(tok):
            if tok is None:
                return
            k, v = tok
            if deps.get(k, 0) < v:
                deps[k] = v

        for t in R:
            add(t.w)
            if t.x:
                for k, v in t.r.items():
                    if k != eng:
                        add((k, v))
        for t in W:
            add(t.w)
            for k, v in t.r.items():
                add((k, v))
        return deps

    def _wait(self, eng, deps):
        for k, v in deps.items():
            if k == eng and (eng == "pe" or eng == "sp" or not SAME_ENGINE_SYNC):
                continue
            if k in self.dma_keys:
                v = self.cnt[k]
            if self.waited.get((eng, k), 0) >= v:
                continue
            self.eng[eng].wait_ge(self.sem[k], v)
            self.waited[(eng, k)] = v

    def _mark(self, tok, R, W):
        k, v = tok
        for t in R:
            if t.r.get(k, 0) < v:
                t.r[k] = v
        for t in W:
            t.w = tok
            t.r = {}

    def op(self, eng, fn, R=(), W=()):
        self._wait(eng, self._deps(R, W, eng))
        ins = fn(self.eng[eng])
        self.cnt[eng] += 1
        ins.then_inc(self.sem[eng], 1)
        self._mark((eng, self.cnt[eng]), R, W)

    def dma(self, q, key, out, in_, R=(), W=(), **kw):
        key = self.dma_sem(key)
        self._wait(q, self._deps(R, W))
        ins = self.eng[q].dma_start(out=out, in_=in_, **kw)
        self.cnt[key] += 16
        ins.then_inc(self.sem[key], 16)
        self._mark((key, self.cnt[key]), R, W)

    def transfer(self, src, dst):
        for d in dst:
            for sr in src:
                toks = list(sr.r.items())
                if sr.w is not None:
                    toks.append(sr.w)
                for k, v in toks:
                    if d.r.get(k, 0) < v:
                        d.r[k] = v

    def barrier(self):
        for e in self.eng:
            for k in list(self.sem.keys()):
                if k == e or self.cnt[k] == 0:
                    continue
                if self.waited.get((e, k), 0) >= self.cnt[k]:
                    continue
                self.eng[e].wait_ge(self.sem[k], self.cnt[k])
                self.waited[(e, k)] = self.cnt[k]


def build(n_ptiles=SEQ // TP, do_sample=True):
    nc = bass.Bass("TRN2", target_bir_lowering=False)
    es = ExitStack()
    with es:
        _build(nc, es, n_ptiles, do_sample)
    return nc


def _build(nc, es, n_ptiles, do_sample):
    S = Sched(nc, es)

    def din(name, shape, dt=F32):
        return nc.dram_tensor(name, list(shape), dt, kind="ExternalInput").ap()

    def dout(name, shape, dt=F32):
        return nc.dram_tensor(name, list(shape), dt, kind="ExternalOutput").ap()

    xp = din("xp", [2, 128, 8, SEQ])
    xs = din("xs", [128, 8, 256])
    cT = din("cT", [128, 8, 6])
    sp_in = din("sp_in", [128, 16, 4, 15])
    sc_in = din("sc_in", [128, 32, 4, 3])
    sr_in = din("sr_in", [4, 128, 16, 128])
    vec_d = din("vec", [128, NV])
    hv_d = din("hv", [64, 32])
    cm_d = din("cm", [64, 1024])
    rc_d = din("rc", [128, 64])
    ada_w = din("ada_w", [2, D, 3 * D])
    pool_w_in = din("pool_w_in", [D, 4096])
    pool_w_group = din("pool_w_group", [4, 512, 512])
    pool_w_out = din("pool_w_out", [2048, D])
    gdn_w_in = din("gdn_w_in", [D, 6176])
    gdn_w_out = din("gdn_w_out", [2048, D])

    yp = dout("yp", [2, 128, 8, SEQ])
    ys = dout("ys", [128, 8, 256])
    pool_o = dout("pool_o", [6, 128, 16, 15])
    conv_o = dout("conv_o", [6, 128, 32, 3])
    rec_o = dout("rec_o", [6, 128, 16, 128])

    wsc = nc.dram_tensor("wsc", [NBLK, 128, 2048], BF16, kind="Internal").ap()
    wab_sc = nc.dram_tensor("wab_sc", [128, 8, 32], BF16, kind="Internal").ap()
    wsc_t = [Tk("wsc%d" % b) for b in range(NBLK)]
    wab_sc_t = Tk("wabsc")

    def sb(name, shape, dt=F32):
        return es.enter_context(nc.sbuf_tensor("sb_" + name, list(shape), dt))

    def blk_src(b):
        if b < WG:
            return pool_w_in[:, b * 256:(b + 1) * 256].rearrange("(k p) c -> p k c", p=128), 8, 256
        if b < PW_OUT:
            return pool_w_group[b - WG].rearrange("(k p) c -> p k c", p=128), 4, 512
        if b < GW_IN:
            j = b - PW_OUT
            return pool_w_out[:, j * 128:(j + 1) * 128].rearrange("(k p) c -> p k c", p=128), 16, 128
        if b < GW_OUT:
            j = b - GW_IN
            return gdn_w_in[:, j * 256:(j + 1) * 256].rearrange("(k p) c -> p k c", p=128), 8, 256
        j = b - GW_OUT
        return gdn_w_out[:, j * 128:(j + 1) * 128].rearrange("(k p) c -> p k c", p=128), 16, 128

    GORD = (3, 2, 1, 0)
    l0_blocks = [PW_IN + 8 + 2 * GORD[0], PW_IN + 9 + 2 * GORD[0], PW_IN + 2 * GORD[0], PW_IN + 2 * GORD[0] + 1]
    for gi in range(4):
        if gi < 3:
            gn = GORD[gi + 1]
            l0_blocks += [PW_IN + 8 + 2 * gn, PW_IN + 9 + 2 * gn, PW_IN + 2 * gn, PW_IN + 2 * gn + 1]
        l0_blocks += [WG + GORD[gi]]
    per_tile_blocks = (l0_blocks + [PW_OUT + i for i in range(8)] + [GW_IN + i for i in range(24)] +
                       [GW_OUT + i for i in range(8)])
    assert sorted(per_tile_blocks) == list(range(NBLK))

    blk_shape = {}
    NCASTQ = 6
    for n_, b in enumerate(per_tile_blocks):
        src, kc, cols = blk_src(b)
        blk_shape[b] = (kc, cols)
        dst = wsc[b][:, 0:kc * cols].rearrange("p (k c) -> p k c", k=kc)
        key = "cast%d" % (n_ % NCASTQ)
        if n_ >= NCASTQ:
            S._wait("pool", {key: S.cnt[key]})
        S.dma("pool", key, dst, src, R=(), W=[wsc_t[b]])
    S.dma("pool", "cast0", wab_sc, gdn_w_in[:, 6144:6176].rearrange("(k p) c -> p k c", p=128), R=(), W=[wab_sc_t])

    vec = sb("vec", [128, NV]); vec_t = Tk("vec")
    hv = sb("hv", [64, 32]); hv_t = Tk("hv")
    cm = sb("cm", [64, 1024]); cm_t = Tk("cm")
    rc = sb("rc", [128, 64]); rc_t = Tk("rc")
    cts = sb("cts", [128, 8, 6]); cts_t = Tk("cts")
    wab = sb("wab", [128, 8, 32], BF16); wab_t = Tk("wab")
    S.dma("sp", "const", vec[:, :], vec_d, W=[vec_t])
    S.dma("sp", "const", hv[:, :], hv_d, W=[hv_t])
    S.dma("sp", "const", cm[:, :], cm_d, W=[cm_t])
    S.dma("sp", "const", rc[:, :], rc_d, W=[rc_t])
    S.dma("sp", "const", cts[:, :, :], cT, W=[cts_t])
    S.dma("sp", "const", wab[:, :, :], wab_sc, R=[wab_sc_t], W=[wab_t])
    Utri = cm[:, 0:64]
    negU = cm[:, 64:128]
    ident_f = cm[:, 128:192]
    negstrict = cm[:, 192:256]
    ones64 = cm[:, 256:384]
    NEGrep = cm[:, 384:896]
    dtb = hv[:, 16:32]

    ones_mean = sb("ones_mean", [128, 128], BF16); ones_mean_t = Tk("om")
    ones_one = sb("ones_one", [128, 128], BF16); ones_one_t = Tk("oo")
    ones_hd = sb("ones_hd", [128, 128], BF16); ones_hd_t = Tk("oh")
    ident_b = sb("ident_b", [128, 128], BF16); ident_b_t = Tk("ib")
    ident_s = sb("ident_s", [128, 128]); ident_s_t = Tk("is")
    ident_d = din("ident", [128, 128])
    S.op("pool", lambda e: e.memset(ones_mean[:, :], 1.0 / 1024.0), W=[ones_mean_t])
    S.op("pool", lambda e: e.memset(ones_one[:, :], 1.0), W=[ones_one_t])
    S.op("pool", lambda e: e.memset(ones_hd[:, :], 1.0 / 128.0), W=[ones_hd_t])
    S.dma("sp", "const", ident_s[:, :], ident_d, W=[ident_s_t])
    S.op("dve", lambda e: e.tensor_copy(out=ident_b[:, :], in_=ident_s[:, :]), R=[ident_s_t], W=[ident_b_t])
    negA = sb("negA", [64, 16]); negA_t = Tk("negA")
    S.op("act", lambda e: e.activation(out=negA[:, :], in_=hv[:, 0:16], func=AF.Exp), R=[hv_t], W=[negA_t])
    S.op("dve", lambda e: e.tensor_scalar(out=negA[:, :], in0=negA[:, :], scalar1=-1.0, scalar2=None, op0=ALU.mult),
         R=[negA_t], W=[negA_t])

    ps = es.enter_context(nc.psum_tensor("ps", [128, 8, 512], F32))
    bank_t = [Tk("bank%d" % i, x=True) for i in range(8)]
    bank_i = [0]

    class Bank:
        def __init__(self, i):
            self.i = i
            self.t = bank_t[i]
            self.ap = ps[:, i, :]
            self.bf = ps[:, i, :].bitcast(BF16)

    free_banks = list(range(8))

    def nextbank():
        assert free_banks, "out of PSUM banks"
        return Bank(free_banks.pop(0))

    def rel(b):
        assert b.i not in free_banks
        free_banks.append(b.i)

    def MM(out, lhsT, rhs, start, stop, R, W):
        S.op("pe", lambda e: e.matmul(out, lhsT=lhsT, rhs=rhs, start=start, stop=stop), R=R, W=W)

    def ACT(out, in_, func, R, W, bias=0.0, scale=1.0):
        S.op("act", lambda e: e.activation(out=out, in_=in_, func=func, bias=bias, scale=scale), R=R, W=W)

    def TT(eng, out, in0, in1, op, R, W):
        S.op(eng, lambda e: e.tensor_tensor(out=out, in0=in0, in1=in1, op=op), R=R, W=W)

    def TS(eng, out, in0, s1, s2, op0, op1, R, W):
        if s2 is None:
            S.op(eng, lambda e: e.tensor_scalar(out=out, in0=in0, scalar1=s1, scalar2=None, op0=op0), R=R, W=W)
        else:
            S.op(eng, lambda e: e.tensor_scalar(out=out, in0=in0, scalar1=s1, scalar2=s2, op0=op0, op1=op1), R=R, W=W)

    def STT(eng, out, in0, scalar, in1, op0, op1, R, W):
        S.op(eng, lambda e: e.scalar_tensor_tensor(out=out, in0=in0, scalar=scalar, in1=in1, op0=op0, op1=op1),
             R=R, W=W)

    kc_ = sb("kconst", [128, 4]); kc_t = Tk("kconst")
    S.op("pool", lambda e: e.memset(kc_[:, 0:1], EPS), W=[kc_t])
    S.op("pool", lambda e: e.memset(kc_[:, 1:2], -0.5), W=[kc_t])
    S.op("pool", lambda e: e.memset(kc_[:, 2:3], -1.0), W=[kc_t])
    S.op("pool", lambda e: e.memset(kc_[:, 3:4], 0.0), W=[kc_t])
    S.op("pool", lambda e: e.memset(kc_[0:8, 3:4], float(np.log(128.0 ** -0.5))), W=[kc_t])
    epsb = kc_[:, 0:1]

    def POW(out, cidx, R, W):
        npart = out.shape[0]
        ex = kc_[0:npart, cidx:cidx + 1]
        if len(out.shape) == 3:
            ex = ex.unsqueeze(2)
        TT("pool", out, out, ex.broadcast_to(list(out.shape)), ALU.pow, R=list(R) + [kc_t], W=W)

    def RSQ(out, in_, R, W):
        npart = out.shape[0]
        ACT(out, in_, AF.Ln, R=list(R) + [kc_t], W=W, bias=epsb[0:npart, :])
        ACT(out, out, AF.Exp, R=W, W=W, scale=-0.5)

    def CP(eng, out, in_, R, W):
        S.op(eng, lambda e: e.tensor_copy(out=out, in_=in_), R=R, W=W)

    def run_il(gens, depth):
        gens = iter(gens)
        active = []
        while True:
            while len(active) < depth:
                g = next(gens, None)
                if g is None:
                    break
                active.append(g)
            if not active:
                return
            for g in list(active):
                try:
                    next(g)
                except StopIteration:
                    active.remove(g)

    def run_pair(a, b, ratio):
        a_done = a is None
        b_done = b is None
        while not (a_done and b_done):
            if not a_done:
                try:
                    next(a)
                except StopIteration:
                    a_done = True
            for _ in range(ratio if not a_done else 1000000):
                if b_done:
                    break
                try:
                    next(b)
                except StopIteration:
                    b_done = True

    def chain_gens(gens):
        for g in gens:
            for _ in g:
                yield

    mod = sb("mod", [128, 2, 24, 6]); mod_t = Tk("mod")
    gs = sb("gs", [128, 2, 8, 6]); gs_t = Tk("gs")
    with ExitStack() as es2:
        aw = [es2.enter_context(nc.sbuf_tensor("aw%d" % i, [128, 8, 512], F32)) for i in range(2)]
        aw_t = [Tk("aw0"), Tk("aw1")]
        ACT(cts[:, :, :], cts[:, :, :], AF.Silu, R=[cts_t], W=[cts_t])
        for l in range(2):
            bk = nextbank()
            for j in range(6):
                sl = (l * 6 + j) % 2
                S.dma("sp", "aw%d" % sl, aw[sl][:, :, :],
                      ada_w[l][:, j * 512:(j + 1) * 512].rearrange("(k p) c -> p k c", p=128), W=[aw_t[sl]])
                for ff in range(4):
                    f = j * 4 + ff
                    for k in range(8):
                        MM(bk.ap[:, f * 6:(f + 1) * 6], aw[sl][:, k, ff * 128:(ff + 1) * 128], cts[:, k, :],
                           k == 0, k == 7, R=[aw_t[sl], cts_t], W=[bk.t])
            TT("dve", mod[:, l, :, :], bk.ap[:, 0:144].rearrange("p (f s) -> p f s", s=6),
               vec[:, V_ADAB + l * 24:V_ADAB + (l + 1) * 24].unsqueeze(2).broadcast_to([128, 24, 6]), ALU.add,
               R=[bk.t, vec_t], W=[mod_t])
            rel(bk)
            STT("dve", gs[:, l, :, :], mod[:, l, 8:16, :], 1.0,
                vec[:, V_GAIN + l * 8:V_GAIN + (l + 1) * 8].unsqueeze(2).broadcast_to([128, 8, 6]),
                ALU.add, ALU.mult, R=[mod_t, vec_t], W=[gs_t])
        S.barrier()

    def shift_ap(l, k, sid):
        return mod[:, l, k, sid:sid + 1]

    def gate_ap(l, k, sid):
        return mod[:, l, 16 + k, sid:sid + 1]

    def gs_ap(l, k, sid):
        return gs[:, l, k, sid:sid + 1]

    NPIPE = 3
    xt = sb("xt", [128, 8, TP]); x_t = [Tk("x%d" % k) for k in range(8)]
    hT = sb("hT", [128, 8, TP], BF16); h_t = [Tk("h%d" % k) for k in range(8)]
    sz = sb("sz", [128, 16, TP], BF16); sz_t = [Tk("sz%d" % k) for k in range(16)]
    rstd = sb("rstd", [128, TP]); rstd_t = Tk("rstd")
    sq = [sb("sq%d" % i, [128, TP], BF16) for i in range(NPIPE)]; sq_t = [Tk("sq%d" % i) for i in range(NPIPE)]
    tmpf = [sb("tmpf%d" % i, [128, TP]) for i in range(NPIPE)]; tmpf_t = [Tk("tf%d" % i) for i in range(NPIPE)]
    NSLOT = 4
    wslot = [sb("wslot%d" % i, [128, 2048], BF16) for i in range(NSLOT)]
    wslot_t = [Tk("ws%d" % i) for i in range(NSLOT)]
    uhist = sb("uhist", [128, 16, 4, 15]); uhist_t = [Tk("uh%d" % g) for g in range(4)]
    chist = sb("chist", [128, 32, 4, 3]); chist_t = [Tk("ch%d" % f) for f in range(32)]
    qT = sb("qT", [128, 8, TP], BF16); q_t = [Tk("q%d" % k) for k in range(8)]
    kT = sb("kT", [128, 8, TP], BF16); k_t = [Tk("k%d" % k) for k in range(8)]
    vT = sb("vT", [128, 16, TP], BF16); v_t = [Tk("v%d" % k) for k in range(16)]
    UB = max(4 * (15 + TP), 4 * 4 * (15 + 64))
    AW = 2 * UB
    assert AW >= 2048
    arA = sb("arA", [128, AW])
    arB = sb("arB", [128, 2048])
    arC = sb("arC", [128, 2048])
    arD = sb("arD", [128, 4 * TP], BF16)
    arE = sb("arE", [128, 2048]); arE_t = Tk("arE")
    ubuf = [arA[:, 0:UB], arA[:, UB:2 * UB]]; ubuf_t = [Tk("ubuf0"), Tk("ubuf1")]
    tmpA = arB; tmpA_t = Tk("tmpA")
    tmpB = arC; tmpB_t = Tk("tmpB")
    dT = arD[:, 0:4 * TP].rearrange("p (a t) -> p a t", a=4); dT_t = Tk("dT")
    raw = [sb("raw%d" % i, [128, 3 + TP + 13]) for i in range(NPIPE)]; raw_t = [Tk("raw%d" % i) for i in range(NPIPE)]
    acc = [sb("acc%d" % i, [128, TP]) for i in range(NPIPE)]; acc_t = [Tk("acc%d" % i) for i in range(NPIPE)]
    ones_sel = sb("ones_sel", [128, 16, 16], BF16); ones_sel_t = Tk("osel")
    S.op("pool", lambda e: e.memset(ones_sel[:, :, :], 0.0), W=[ones_sel_t])
    for f_ in range(16):
        S.op("pool", lambda e: e.memset(ones_sel[:, f_, f_:f_ + 1], 1.0), W=[ones_sel_t])
    rinv16 = sb("rinv16", [16, TP]); rinv16_t = Tk("rinv16")
    rhi = sb("rhi", [16, TP], BF16); rhi_t = Tk("rhi")
    rlo = sb("rlo", [16, TP], BF16); rlo_t = Tk("rlo")
    sel16 = sb("sel16", [16, 16, 128], BF16); sel16_t = Tk("sel16")
    S.op("dve", lambda e: e.tensor_copy(out=sel16[:, :, :],
                                        in_=ident_s[0:16, 0:16].unsqueeze(2).broadcast_to([16, 16, 128])),
         R=[ident_s_t], W=[sel16_t])
    Sst = sb("Sst", [128, 16, 128]); Sst_t = Tk("S")
    Sbf = [sb("Sbf%d" % i, [128, 16, 128], BF16) for i in range(2)]; Sbf_t = [Tk("Sbf0"), Tk("Sbf1")]
    sm = [sb("sm%d" % i, [128, 8, 16]) for i in range(2)]
    sm_t = [[Tk("sm%d_%d" % (i, j)) for j in range(8)] for i in range(2)]
    FX = arA[0:64, 0:1024].rearrange("p (a c) -> p a c", a=16); FX_t = Tk("FX")
    FG = arA[0:64, 1024:2048].rearrange("p (a c) -> p a c", a=16); FG_t = Tk("FG")
    FD = arB[0:64, 0:1024].rearrange("p (a c) -> p a c", a=16); FD_t = Tk("FD")
    FN = arB[0:64, 1024:2048].rearrange("p (a c) -> p a c", a=16); FN_t = Tk("FN")
    F1 = arC[0:64, 0:1024].rearrange("p (a c) -> p a c", a=16); F1_t = Tk("F1")
    FE = arC[:, 1024:2048].rearrange("p (a c) -> p a c", a=16); FE_t = Tk("FE")
    NP2 = [sb("NP%d" % i, [64, 16, 2, 64], F32) for i in range(2)]
    Nb2 = [sb("Nb%d" % i, [64, 16, 64], F32) for i in range(2)]
    NT_t2 = [[Tk("NTa%d" % i), Tk("NTb%d" % i)] for i in range(2)]
    P_t2 = [[Tk("Pa%d" % i), Tk("Pb%d" % i)] for i in range(2)]
    N_t2 = [[Tk("Na%d" % i), Tk("Nb%d" % i)] for i in range(2)]
    TTb = [sb("TTb%d" % i, [64, 16, 64], BF16) for i in range(2)]; TTb_t = [Tk("TTb0"), Tk("TTb1")]
    ATb = [sb("AT%d" % i, [64, 16, 64], BF16) for i in range(2)]; AT_t = [Tk("AT0"), Tk("AT1")]
    qd = [sb("qd%d" % i, [128, 16, 64], BF16) for i in range(2)]; qd_t = [Tk("qd0"), Tk("qd1")]
    Ktok = sb("Ktok", [64, 8, 128], BF16); Ktok_t = Tk("Ktok")
    Vtok = sb("Vtok", [64, 16, 128], BF16); Vtok_t = Tk("Vtok")
    Wn = arE[0:64, :].rearrange("p (a e) -> p a e", a=16); Wn_t = arE_t
    rso = arE[:, 0:1024].rearrange("p (a c) -> p a c", a=16); rso_t = arE_t
    og = arE[:, 1024:2048].rearrange("p (a c) -> p a c", a=16); og_t = arE_t
    Wp = sb("Wp", [64, 16, 128], BF16); Wp_t = Tk("Wp")
    vnew = sb("vnew", [64, 16, 128], BF16); vnew_t = Tk("vnew")
    vnew2 = sb("vnew2", [64, 16, 128], BF16); vnew2_t = Tk("vnew2")
    osq = sb("osq", [128, 16, 64], BF16); osq_t = Tk("osq")

    n_tiles_total = 2 * n_ptiles + (1 if do_sample else 0)
    wseq = per_tile_blocks * n_tiles_total
    wstate = {"use": 0, "load": 0, "cache": {}}

    def w_issue_loads(upto):
        while wstate["load"] < min(upto, len(wseq)):
            i = wstate["load"]
            b = wseq[i]
            kc, cols = blk_shape[b]
            sl = i % NSLOT
            S.dma("sp", "w%d" % sl, wslot[sl][:, 0:kc * cols], wsc[b][:, 0:kc * cols], R=[wsc_t[b]], W=[wslot_t[sl]])
            wstate["load"] += 1

    def get_blk(b):
        c = wstate["cache"]
        if b in c:
            return c[b]
        i = wstate["use"]
        assert wseq[i] == b, (i, wseq[i], b)
        w_issue_loads(i + NSLOT)
        kc, cols = blk_shape[b]
        sl = i % NSLOT
        wstate["use"] += 1
        c.clear()
        c[b] = (wslot[sl][:, 0:kc * cols].rearrange("p (k c) -> p k c", k=kc), wslot_t[sl])
        return c[b]

    def rms_rstd(T):
        bk = nextbank()
        for k in range(8):
            i = k % NPIPE
            ACT(sq[i][:, :T], xt[:, k, :T], AF.Square, R=[x_t[k]], W=[sq_t[i]])
            MM(bk.ap[:, :T], ones_mean[:, :], sq[i][:, :T], k == 0, k == 7, R=[ones_mean_t, sq_t[i]], W=[bk.t])
        RSQ(rstd[:, :T], bk.ap[:, :T], R=[bk.t], W=[rstd_t])
        rel(bk)

    def make_h(l, T, segs):
        rms_rstd(T)
        for k in range(8):
            i = k % NPIPE
            tf = tmpf[i]
            TT("dve", tf[:, :T], xt[:, k, :T], rstd[:, :T], ALU.mult, R=[x_t[k], rstd_t], W=[tmpf_t[i]])
            for (c0, c1, sid) in segs:
                ACT(hT[:, k, c0:c1], tf[:, c0:c1], AF.Identity, R=[tmpf_t[i], gs_t, mod_t], W=[h_t[k]],
                    bias=shift_ap(l, k, sid), scale=gs_ap(l, k, sid))

    def proj(wv, wt, col0, T):
        bk = nextbank()
        for k in range(8):
            MM(bk.ap[:, :T], wv[:, k, col0:col0 + 128], hT[:, k, :T], k == 0, k == 7, R=[wt, h_t[k]], W=[bk.t])
        return bk

    def zproj_gen(blk, col0, fz, T):
        wv, wt = get_blk(blk)
        bk = proj(wv, wt, col0, T)
        yield
        ACT(sz[:, fz, :T], bk.ap[:, :T], AF.Silu, R=[bk.t], W=[sz_t[fz]])
        rel(bk)
        yield

    def outp_gen(l, blk, fo, T, segs):
        wv, wt = get_blk(blk)
        bk = nextbank()
        for k in range(16):
            MM(bk.ap[:, :T], wv[:, k, :], sz[:, k, :T], k == 0, k == 15, R=[wt, sz_t[k]], W=[bk.t])
        yield
        for (c0, c1, sid) in segs:
            STT("dve", xt[:, fo, c0:c1], bk.ap[:, c0:c1], gate_ap(l, fo, sid), xt[:, fo, c0:c1],
                ALU.mult, ALU.add, R=[bk.t, mod_t, x_t[fo]], W=[x_t[fo]])
        rel(bk)
        yield

    sb_cur = [0]

    def do_tile(seqs, n_seg, L, x_src, y_dst, first, last, prompt, x_loaded=False, next_x=None):
        T = n_seg * L
        C = T // 64
        segs = [(s * L, (s + 1) * L, seqs[s]) for s in range(n_seg)]
        wstate["cache"].clear()
        if not x_loaded:
            for k in range(8):
                S.dma("sp", "xld", xt[:, k, :T], x_src[:, k, :], W=[x_t[k]])
        if first:
            if prompt:
                for g in range(4):
                    S.op("pool", lambda e: e.memset(uhist[:, 4 * g:4 * g + 4, :, :], 0.0), W=[uhist_t[g]])
                S.op("pool", lambda e: e.memset(chist[:, :, :, :], 0.0), W=chist_t)
                S.op("pool", lambda e: e.memset(Sst[:, :, :], 0.0), W=[Sst_t])
                S.op("pool", lambda e: e.memset(Sbf[sb_cur[0]][:, :, :], 0.0), W=[Sbf_t[sb_cur[0]]])
            else:
                S.dma("pool", "hld", uhist[:, :, :, :], sp_in, W=uhist_t)
                S.dma("pool", "hld", chist[:, :, :, :], sc_in, W=chist_t)
        S.transfer([FX_t, FG_t], ubuf_t)
        S.transfer([FD_t, FN_t], [tmpA_t])
        S.transfer([F1_t, FE_t], [tmpB_t])

        make_h(0, T, segs)
        EL = 15 + L

        def v4(ap):
            return ap[:, 0:4 * n_seg * EL].rearrange("p (a s t) -> p a s t", a=4, s=n_seg)

        ub4 = [v4(ubuf[0]), v4(ubuf[1])]
        tA4 = v4(tmpA)
        tB4 = v4(tmpB)

        def z0(j4):
            gens = []
            for ff in range(4):
                fz = j4 * 4 + ff
                gens.append(zproj_gen(PW_IN + 8 + fz // 2, (fz % 2) * 128, fz, T))
            run_il(gens, 4)

        def s1(g, ui):
            u = ub4[ui]
            ut = ubuf_t[ui]
            CP("pool", u[:, :, :, 0:15], uhist[:, 4 * g:4 * g + 4, 0:n_seg, :], R=[uhist_t[g]], W=[ut])
            banks = []
            for fu in range(4):
                ch = 4 * g + fu
                wv, wt = get_blk(PW_IN + ch // 2)
                banks.append(proj(wv, wt, (ch % 2) * 128, T))
            for fu in range(4):
                bk = banks[fu]
                ACT(u[:, fu, :, 15:EL], bk.ap[:, :T].rearrange("p (s t) -> p s t", s=n_seg), AF.Copy,
                    R=[bk.t], W=[ut])
                rel(bk)
            CP("pool", uhist[:, 4 * g:4 * g + 4, 0:n_seg, :], u[:, :, :, L:EL], R=[ut], W=[uhist_t[g]])

        def s3(g, ui):
            w = (2, 4, 8, 16)[g]
            u = ub4[ui]
            ut = ubuf_t[ui]
            src, src_t = u, ut
            dsts = [(tA4, tmpA_t), (tB4, tmpB_t)]
            sh = 1
            di = 0
            while sh < w:
                dst, dst_t = dsts[di]
                lo = 2 * sh - 1
                TT("dve", dst[:, :, :, lo:EL], src[:, :, :, lo:EL], src[:, :, :, lo - sh:EL - sh],
                   ALU.add, R=[src_t], W=[dst_t])
                src, src_t = dst, dst_t
                di = 1 - di
                sh *= 2
            d4 = dT[:, :, :T].rearrange("p a (s t) -> p a s t", s=n_seg)
            STT("dve", d4, src[:, :, :, 15:EL], 1.0 / w, u[:, :, :, 15:EL], ALU.mult, ALU.subtract,
                R=[src_t, ut], W=[dT_t])
            if first and prompt:
                nfix = w - 1
                tfx = tmpf[0][:, 0:4 * nfix].rearrange("p (a t) -> p a t", a=4)
                TT("dve", tfx, src[:, :, 0, 15:15 + nfix],
                   rc[:, g * 16:g * 16 + nfix].unsqueeze(1).broadcast_to([128, 4, nfix]), ALU.mult,
                   R=[src_t, rc_t], W=[tmpf_t[0]])
                TT("dve", dT[:, :, 0:nfix], tfx, u[:, :, 0, 15:15 + nfix], ALU.subtract,
                   R=[tmpf_t[0], ut], W=[dT_t])

        def s4(g):
            wgv, wgt = get_blk(WG + g)
            banks = []
            for fo in range(4):
                bk = nextbank()
                banks.append(bk)
                for kk in range(4):
                    MM(bk.ap[:, :T], wgv[:, kk, fo * 128:(fo + 1) * 128], dT[:, kk, :T], kk == 0, kk == 3,
                       R=[wgt, dT_t], W=[bk.t])
            for fo in range(4):
                ch = 4 * g + fo
                bk = banks[fo]
                STT("dve", sz[:, ch, :T], bk.ap[:, :T], vec[:, V_PSCALE + ch:V_PSCALE + ch + 1], sz[:, ch, :T],
                    ALU.mult, ALU.mult, R=[bk.t, vec_t, sz_t[ch]], W=[sz_t[ch]])
                rel(bk)

        z0(GORD[0])
        s1(GORD[0], 0)
        for gi in range(4):
            if gi < 3:
                z0(GORD[gi + 1])
            s3(GORD[gi], gi % 2)
            if gi < 3:
                s1(GORD[gi + 1], (gi + 1) % 2)
            s4(GORD[gi])
        if last:
            for (c0, c1, sid) in segs:
                s = c0 // L
                S.dma("pool", "ost", pool_o[sid], uhist[:, :, s, :], R=uhist_t, W=[])
        run_il([outp_gen(0, PW_OUT + fo, fo, T, segs) for fo in range(8)], 4)

        make_h(1, T, segs)

        pipe_free = list(range(NPIPE))
        ssb = nextbank()

        def qkv_gen(f):
            i = pipe_free.pop(0)
            wv, wt = get_blk(GW_IN + f // 2)
            bk = proj(wv, wt, (f % 2) * 128, T)
            yield
            rw, rwt = raw[i], raw_t[i]
            ac, act_ = acc[i], acc_t[i]
            r3 = rw[:, 0:n_seg * (3 + L)].rearrange("p (s t) -> p s t", s=n_seg)
            a3 = ac[:, :T].rearrange("p (s t) -> p s t", s=n_seg)
            b3 = bk.ap[:, :T].rearrange("p (s t) -> p s t", s=n_seg)
            CP("pool", r3[:, :, 0:3], chist[:, f, 0:n_seg, :], R=[chist_t[f]], W=[rwt])
            ACT(r3[:, :, 3:3 + L], b3, AF.Copy, R=[bk.t], W=[rwt])
            ACT(a3, b3, AF.Identity, R=[bk.t, vec_t], W=[act_],
                scale=vec[:, V_CONVW + 3 * 32 + f:V_CONVW + 3 * 32 + f + 1])
            rel(bk)
            yield
            CP("pool", chist[:, f, 0:n_seg, :], r3[:, :, L:L + 3], R=[rwt], W=[chist_t[f]])
            for tp in range(3):
                cw = vec[:, V_CONVW + tp * 32 + f:V_CONVW + tp * 32 + f + 1]
                STT("dve", a3, r3[:, :, tp:tp + L], cw, a3, ALU.mult, ALU.add, R=[rwt, vec_t, act_], W=[act_])
                yield
            if f < 16:
                ACT(ac[:, :T], ac[:, :T], AF.Silu, R=[act_], W=[act_])
                yield
                ACT(sq[i][:, :T], ac[:, :T], AF.Square, R=[act_], W=[sq_t[i]])
                MM(ssb.ap[0:16, :T], ones_sel[:, f, :], sq[i][:, :T], f == 0, f == 15, R=[ones_sel_t, sq_t[i]], W=[ssb.t])
                dst, dst_t = (qT[:, f, :T], q_t[f]) if f < 8 else (kT[:, f - 8, :T], k_t[f - 8])
                CP("dve", dst, ac[:, :T], R=[act_], W=[dst_t])
            else:
                ACT(vT[:, f - 16, :T], ac[:, :T], AF.Silu, R=[act_], W=[v_t[f - 16]])
            pipe_free.append(i)
            yield

        z1_gens = [zproj_gen(GW_IN + 16 + fz // 2, (fz % 2) * 128, fz, T) for fz in range(16)]
        run_il([qkv_gen(f) for f in range(32)] + z1_gens, NPIPE)
        ACT(rinv16[:, :T], ssb.ap[0:16, :T], AF.Ln, R=[ssb.t, kc_t], W=[rinv16_t], bias=epsb[0:16, :])
        rel(ssb)
        ACT(rinv16[:, :T], rinv16[:, :T], AF.Exp, R=[rinv16_t, kc_t], W=[rinv16_t], scale=-0.5, bias=kc_[0:16, 3:4])
        CP("dve", rhi[:, :T], rinv16[:, :T], R=[rinv16_t], W=[rhi_t])
        TT("dve", rlo[:, :T], rinv16[:, :T], rhi[:, :T], ALU.subtract, R=[rinv16_t, rhi_t], W=[rlo_t])
        for f in range(16):
            bq = nextbank()
            oh = sel16[:, f, :]
            MM(bq.ap[:, :T], oh, rhi[:, :T], True, False, R=[sel16_t, rhi_t], W=[bq.t])
            MM(bq.ap[:, :T], oh, rlo[:, :T], False, True, R=[sel16_t, rlo_t], W=[bq.t])
            dst, dst_t = (qT[:, f, :T], q_t[f]) if f < 8 else (kT[:, f - 8, :T], k_t[f - 8])
            TT("dve", dst, dst, bq.ap[:, :T], ALU.mult, R=[dst_t, bq.t], W=[dst_t])
            rel(bq)
        if last:
            for (c0, c1, sid) in segs:
                s = c0 // L
                S.dma("pool", "ost", conv_o[sid], chist[:, :, s, :], R=chist_t, W=[])
        S.transfer(ubuf_t, [FX_t, FG_t])
        S.transfer([tmpA_t], [FD_t, FN_t])
        S.transfer([tmpB_t], [F1_t, FE_t])

        def prep(c):
            for _ in front(c):
                yield
            for _ in solve(c):
                yield

        def solve(c):
            p = c % 2
            NP, Nb, NT_t, P_t, N_t = NP2[p], Nb2[p], NT_t2[p], P_t2[p], N_t2[p]
            yield
            for lvl in range(5):
                k1 = lvl == 0
                k16 = lvl == 4
                for half in range(2):
                    hs = slice(half * 8, half * 8 + 8)
                    Rm = [N_t[half], NT_t[half], P_t[half]]
                    a_banks = []
                    if k1 or k16:
                        slot = 0 if k1 else 1
                        bA = nextbank()
                        a_banks.append(bA)
                        for hh in range(8):
                            h = half * 8 + hh
                            MM(bA.ap[0:64, hh * 64:(hh + 1) * 64], Nb[:, h, :], NP[:, h, slot, :], True, True,
                               R=Rm, W=[bA.t])
                    else:
                        for q in range(2):
                            bA = nextbank()
                            a_banks.append(bA)
                            for hh in range(4):
                                h = half * 8 + q * 4 + hh
                                MM(bA.ap[0:64, hh * 128:(hh + 1) * 128], Nb[:, h, :],
                                   NP[:, h, :, :].rearrange("p s c -> p (s c)"), True, True, R=Rm, W=[bA.t])
                    bB = nextbank()
                    for hh in range(8):
                        h = half * 8 + hh
                        MM(bB.ap[0:64, hh * 64:(hh + 1) * 64], NP[:, h, 0, :], Nb[:, h, :], True, True,
                           R=Rm, W=[bB.t])
                    yield
                    if k1:
                        ACT(NP[:, hs, 0, :], a_banks[0].ap[0:64, :].rearrange("p (a c) -> p a c", a=8), AF.Copy,
                            R=[a_banks[0].t], W=[NT_t[half]])
                    elif k16:
                        TT("dve", NP[:, hs, 1, :], a_banks[0].ap[0:64, :].rearrange("p (a c) -> p a c", a=8),
                           NP[:, hs, 1, :], ALU.add, R=[a_banks[0].t, P_t[half]], W=[P_t[half]])
                    else:
                        for q in range(2):
                            h4 = slice(half * 8 + q * 4, half * 8 + q * 4 + 4)
                            b4 = a_banks[q].ap[0:64, :].rearrange("p (a s c) -> p a s c", a=4, s=2)
                            TT("dve", NP[:, h4, 1, :], b4[:, :, 1, :], NP[:, h4, 1, :], ALU.add,
                               R=[a_banks[q].t, P_t[half]], W=[P_t[half]])
                            ACT(NP[:, h4, 0, :], b4[:, :, 0, :], AF.Copy, R=[a_banks[q].t, P_t[half]], W=[NT_t[half]])
                    for bA in a_banks:
                        rel(bA)
                    yield
                    ACT(Nb[:, hs, :], bB.ap[0:64, :].rearrange("p (a c) -> p a c", a=8), AF.Copy,
                        R=[bB.t], W=[N_t[half]])
                    rel(bB)
                    yield
            for half in range(2):
                hs = slice(half * 8, half * 8 + 8)
                bP = nextbank()
                for hh in range(8):
                    h = half * 8 + hh
                    MM(bP.ap[0:64, hh * 64:(hh + 1) * 64], Nb[:, h, :], NP[:, h, 1, :], True, True,
                       R=[N_t[half], P_t[half]], W=[bP.t])
                yield
                TT("dve", TTb[p][:, hs, :], bP.ap[0:64, :].rearrange("p (a c) -> p a c", a=8), NP[:, hs, 1, :],
                   ALU.add, R=[bP.t, P_t[half]], W=[TTb_t[p]])
                rel(bP)
                yield

        def front(c):
            p = c % 2
            NP, Nb, NT_t, P_t, N_t = NP2[p], Nb2[p], NT_t2[p], P_t2[p], N_t2[p]
            tok = slice(c * 64, (c + 1) * 64)
            smp, st = sm[p], sm_t[p]
            g_ = smp[0:64, 0, :]; beta = smp[0:64, 1, :]; gcs = smp[0:64, 2, :]; egc = smp[0:64, 3, :]
            negegc = smp[0:64, 4, :]; ekd = smp[0:64, 5, :]; bekd = smp[0:64, 6, :]; egl = smp[:, 7, :]
            bk = nextbank()
            for k in range(8):
                MM(bk.ap[0:64, 0:32], hT[:, k, tok], wab[:, k, :], k == 0, k == 7, R=[h_t[k], wab_t], W=[bk.t])
            yield
            TT("dve", g_, bk.ap[0:64, 0:16], dtb, ALU.add, R=[bk.t, hv_t], W=[st[0]])
            ACT(beta, bk.ap[0:64, 16:32], AF.Exp, R=[bk.t], W=[st[1]], scale=-1.0)
            rel(bk)
            ACT(g_, g_, AF.Exp, R=[st[0]], W=[st[0]])
            TS("dve", beta, beta, 1.0, None, ALU.add, None, R=[st[1]], W=[st[1]])
            S.op("dve", lambda e: e.reciprocal(out=beta, in_=beta), R=[st[1]], W=[st[1]])
            yield
            ACT(g_, g_, AF.Ln, R=[st[0]], W=[st[0]], bias=1.0)
            yield
            TT("dve", g_, g_, negA[:, :], ALU.mult, R=[st[0], negA_t], W=[st[0]])
            yield
            b2 = nextbank()
            MM(b2.ap[0:64, 0:16], Utri, g_, True, True, R=[cm_t, st[0]], W=[b2.t])
            MM(b2.ap[:, 16:32], ones64, g_, True, True, R=[cm_t, st[0]], W=[b2.t])
            gb3 = g_.unsqueeze(2).broadcast_to([64, 16, 64])
            TT("dve", FX[:, :, :], gb3, Utri.unsqueeze(1).broadcast_to([64, 16, 64]), ALU.mult,
               R=[st[0], cm_t], W=[FX_t])
            yield
            ACT(gcs, b2.ap[0:64, 0:16], AF.Copy, R=[b2.t], W=[st[2]])
            ACT(egc, b2.ap[0:64, 0:16], AF.Exp, R=[b2.t], W=[st[3]])
            ACT(egl, b2.ap[:, 16:32], AF.Exp, R=[b2.t], W=[st[7]])
            ACT(FG[:, :, :], gb3, AF.Copy, R=[st[0]], W=[FG_t])
            yield
            TS("dve", negegc, egc, -1.0, None, ALU.mult, None, R=[st[3]], W=[st[4]])
            TT("dve", ekd, b2.ap[0:64, 16:32], gcs, ALU.subtract, R=[b2.t, st[2]], W=[st[5]])
            rel(b2)
            yield
            ACT(ekd, ekd, AF.Exp, R=[st[5]], W=[st[5]])
            yield
            TT("dve", bekd, ekd, beta, ALU.mult, R=[st[5], st[1]], W=[st[6]])
            for half in range(2):
                hs = slice(half * 8, half * 8 + 8)
                bA = nextbank()
                MM(bA.ap[:, :], ones64, FX[:, hs, :], True, True, R=[cm_t, FX_t], W=[bA.t])
                yield
                ACT(FE[:, hs, :], bA.ap[:, :].rearrange("p (a c) -> p a c", a=8), AF.Exp, R=[bA.t], W=[FE_t])
                rel(bA)
                yield
            TT("dve", qd[p][:, :, :].rearrange("p (a r) c -> p a r c", r=2),
               qT[:, :, tok].unsqueeze(2).broadcast_to([128, 8, 2, 64]),
               FE[:, :, :].rearrange("p (a r) c -> p a r c", r=2), ALU.mult, R=q_t + [FE_t], W=[qd_t[p]])
            yield
            for half in range(2):
                hs = slice(half * 8, half * 8 + 8)
                bD = nextbank()
                MM(bD.ap[0:64, :], ones64[:, 0:64], FX[:, hs, :], True, False, R=[cm_t, FX_t], W=[bD.t])
                MM(bD.ap[0:64, :], negU, FG[:, hs, :], False, False, R=[cm_t, FG_t], W=[bD.t])
                MM(bD.ap[0:64, :], ident_f, NEGrep, False, True, R=[cm_t], W=[bD.t])
                yield
                ACT(FD[:, hs, :], bD.ap[0:64, :].rearrange("p (a c) -> p a c", a=8), AF.Exp, R=[bD.t], W=[FD_t])
                rel(bD)
                yield
            bK = nextbank()
            bQ = nextbank()
            for hk in range(8):
                MM(bK.ap[0:64, hk * 64:(hk + 1) * 64], kT[:, hk, tok], kT[:, hk, tok], True, True,
                   R=[k_t[hk]], W=[bK.t])
                MM(bQ.ap[0:64, hk * 64:(hk + 1) * 64], kT[:, hk, tok], qT[:, hk, tok], True, True,
                   R=[k_t[hk], q_t[hk]], W=[bQ.t])
            TT("dve", FN[:, :, :], beta.unsqueeze(2).broadcast_to([64, 16, 64]),
               negstrict.unsqueeze(1).broadcast_to([64, 16, 64]), ALU.mult, R=[st[1], cm_t], W=[FN_t])
            yield
            FD4 = FD[:, :, :].rearrange("p (a r) c -> p a r c", r=2)
            TT("dve", F1[:, :, :].rearrange("p (a r) c -> p a r c", r=2),
               bK.ap[0:64, :].rearrange("p (a c) -> p a c", a=8).unsqueeze(2).broadcast_to([64, 8, 2, 64]), FD4,
               ALU.mult, R=[bK.t, FD_t], W=[F1_t])
            rel(bK)
            yield
            TT("dve", ATb[p][:, :, :].rearrange("p (a r) c -> p a r c", r=2),
               bQ.ap[0:64, :].rearrange("p (a c) -> p a c", a=8).unsqueeze(2).broadcast_to([64, 8, 2, 64]), FD4,
               ALU.mult, R=[bQ.t, FD_t], W=[AT_t[p]])
            rel(bQ)
            TT("dve", NP[:, :, 0, :], F1[:, :, :], FN[:, :, :], ALU.mult, R=[F1_t, FN_t], W=NT_t)
            yield
            for half in range(2):
                hs = slice(half * 8, half * 8 + 8)
                bT = nextbank()
                for hh in range(8):
                    h = half * 8 + hh
                    S.op("pe", lambda e: e.transpose(bT.ap[0:64, hh * 64:(hh + 1) * 64], NP[:, h, 0, :],
                                                     ident_s[0:64, 0:64]), R=[NT_t[half], ident_s_t], W=[bT.t])
                yield
                ACT(Nb[:, hs, :], bT.ap[0:64, :].rearrange("p (a c) -> p a c", a=8), AF.Copy,
                    R=[bT.t], W=[N_t[half]])
                rel(bT)
                TT("pool", NP[:, hs, 1, :], NP[:, hs, 0, :], ident_f.unsqueeze(1).broadcast_to([64, 8, 64]),
                   ALU.add, R=[NT_t[half], cm_t], W=[P_t[half]])
                yield

        def chain(c):
            p = c % 2
            tok = slice(c * 64, (c + 1) * 64)
            seg = (c * 64) // L
            sid = seqs[seg]
            smp, st = sm[p], sm_t[p]
            beta = smp[0:64, 1, :]; negegc = smp[0:64, 4, :]; bekd = smp[0:64, 6, :]; egl = smp[:, 7, :]
            so = sb_cur[0]
            sn = 1 - so
            if not prompt:
                S.dma("pool", "sld", Sst[:, :, :], sr_in[seg], W=[Sst_t])
                ACT(Sbf[so][:, :, :], Sst[:, :, :], AF.Copy, R=[Sst_t], W=[Sbf_t[so]])
            bKt = nextbank()
            for hk in range(8):
                S.op("pe", lambda e: e.transpose(bKt.bf[0:64, hk * 128:(hk + 1) * 128], kT[:, hk, tok], ident_b[:, :]),
                     R=[k_t[hk], ident_b_t], W=[bKt.t])
            ACT(Ktok[:, :, :], bKt.bf[0:64, 0:1024].rearrange("p (a c) -> p a c", a=8), AF.Copy, R=[bKt.t], W=[Ktok_t])
            rel(bKt)
            for half in range(2):
                bV = nextbank()
                for hh in range(8):
                    h = half * 8 + hh
                    S.op("pe", lambda e: e.transpose(bV.bf[0:64, hh * 128:(hh + 1) * 128], vT[:, h, tok],
                                                     ident_b[:, :]), R=[v_t[h], ident_b_t], W=[bV.t])
                ACT(Vtok[:, half * 8:half * 8 + 8, :], bV.bf[0:64, 0:1024].rearrange("p (a c) -> p a c", a=8),
                    AF.Copy, R=[bV.t], W=[Vtok_t])
                rel(bV)
            yield
            TT("pool", Sst[:, :, :], Sst[:, :, :], egl.unsqueeze(2).broadcast_to([128, 16, 128]), ALU.mult,
               R=[Sst_t, st[7]], W=[Sst_t])
            for q4 in range(4):
                h4 = slice(q4 * 4, q4 * 4 + 4)
                bS = nextbank()
                for hp in range(2):
                    hk = q4 * 2 + hp
                    MM(bS.ap[0:64, hp * 256:(hp + 1) * 256], kT[:, hk, tok], Sbf[so][:, 2 * hk:2 * hk + 2, :], True, True,
                       R=[k_t[hk], Sbf_t[so]], W=[bS.t])
                TT("dve", Wn[:, h4, :], bS.ap[0:64, :].rearrange("p (a e) -> p a e", a=4),
                   negegc[:, h4].unsqueeze(2).broadcast_to([64, 4, 128]), ALU.mult, R=[bS.t, st[4]], W=[Wn_t])
                rel(bS)
                TT("dve", Wp[:, h4, :], Wn[:, h4, :], Vtok[:, h4, :], ALU.add, R=[Wn_t, Vtok_t], W=[Wp_t])
                yield
            for q4 in range(4):
                h4 = slice(q4 * 4, q4 * 4 + 4)
                bU = nextbank()
                for hh in range(4):
                    h = q4 * 4 + hh
                    MM(bU.ap[0:64, hh * 128:(hh + 1) * 128], TTb[p][:, h, :], Wp[:, h, :], True, True,
                       R=[TTb_t[p], Wp_t], W=[bU.t])
                bU3 = bU.ap[0:64, :].rearrange("p (a e) -> p a e", a=4)
                TT("dve", vnew2[:, h4, :], bU3, bekd[:, h4].unsqueeze(2).broadcast_to([64, 4, 128]), ALU.mult,
                   R=[bU.t, st[6]], W=[vnew2_t])
                TT("dve", vnew[:, h4, :], bU3, beta[:, h4].unsqueeze(2).broadcast_to([64, 4, 128]), ALU.mult,
                   R=[bU.t, st[1]], W=[vnew_t])
                rel(bU)
                yield
            for q4 in range(4):
                h4 = slice(q4 * 4, q4 * 4 + 4)
                bS2 = nextbank()
                for hp in range(2):
                    hk = q4 * 2 + hp
                    MM(bS2.ap[:, hp * 256:(hp + 1) * 256], Ktok[:, hk, :], vnew2[:, 2 * hk:2 * hk + 2, :], True, True,
                       R=[Ktok_t, vnew2_t], W=[bS2.t])
                TT("dve", Sst[:, h4, :], bS2.ap[:, :].rearrange("p (a e) -> p a e", a=4), Sst[:, h4, :], ALU.add,
                   R=[bS2.t, Sst_t], W=[Sst_t])
                rel(bS2)
                yield
            ACT(Sbf[sn][:, :, :], Sst[:, :, :], AF.Copy, R=[Sst_t], W=[Sbf_t[sn]])
            if (not prompt) or (last and c == C - 1):
                S.dma("pool", "ost", rec_o[sid], Sst[:, :, :], R=[Sst_t], W=[])
            yield
            for half in range(2):
                hs = slice(half * 8, half * 8 + 8)
                bO = nextbank()
                for hh in range(8):
                    h = half * 8 + hh
                    MM(bO.ap[:, hh * 64:(hh + 1) * 64], Sbf[so][:, h, :], qd[p][:, h, :], True, False,
                       R=[Sbf_t[so], qd_t[p]], W=[bO.t])
                    MM(bO.ap[:, hh * 64:(hh + 1) * 64], vnew[:, h, :], ATb[p][:, h, :], False, True,
                       R=[vnew_t, AT_t[p]], W=[bO.t])
                bO3 = bO.ap[:, :].rearrange("p (a c) -> p a c", a=8)
                ACT(osq[:, hs, :], bO3, AF.Square, R=[bO.t], W=[osq_t])
                yield
                bM = nextbank()
                MM(bM.ap[:, :], ones_hd[:, :], osq[:, hs, :], True, True, R=[ones_hd_t, osq_t], W=[bM.t])
                yield
                RSQ(rso[:, hs, :], bM.ap[:, :].rearrange("p (a c) -> p a c", a=8), R=[bM.t], W=[rso_t])
                rel(bM)
                yield
                STT("dve", og[:, hs, :], bO3, vec[:, V_NORMW:V_NORMW + 1], rso[:, hs, :], ALU.mult, ALU.mult,
                    R=[bO.t, vec_t, rso_t], W=[og_t])
                rel(bO)
                yield
                TT("pool", sz[:, hs, tok], og[:, hs, :], sz[:, hs, tok], ALU.mult,
                   R=[og_t] + sz_t[half * 8:half * 8 + 8], W=sz_t[half * 8:half * 8 + 8])
                yield
            sb_cur[0] = sn

        def step(g):
            if g is None:
                return True
            try:
                next(g)
                return False
            except StopIteration:
                return True

        def run2(a, b, ratio):
            a_done = a is None
            while not a_done:
                a_done = step(a)
                for _ in range(ratio):
                    if step(b):
                        b = None
                        break
            return b

        def drain(g):
            while not step(g):
                pass

        drain(front(0))
        sv = solve(0)
        sv = run2(front(1) if C > 1 else None, sv, 2)
        drain(sv)
        for c in range(C):
            sv = solve(c + 1) if c + 1 < C else None
            sv = run2(chain(c), sv, 3)
            sv = run2(front(c + 2) if c + 2 < C else None, sv, 2)
            drain(sv)

        run_il([outp_gen(1, GW_OUT + fo, fo, T, segs) for fo in range(8)], 4)
        rms_rstd(T)
        for k in range(8):
            i = k % NPIPE
            STT("dve", tmpf[i][:, :T], xt[:, k, :T], vec[:, V_FGAIN + k:V_FGAIN + k + 1], rstd[:, :T], ALU.mult, ALU.mult,
                R=[x_t[k], vec_t, rstd_t], W=[tmpf_t[i]])
            S.dma("sp", "yst", y_dst[:, k, :], tmpf[i][:, :T], R=[tmpf_t[i]], W=[])
            if next_x is not None:
                S.dma("sp", "xld", xt[:, k, :next_x.shape[2]], next_x[:, k, :], W=[x_t[k]])

    tiles = []
    for s in range(2):
        for ti in range(n_ptiles):
            t0 = ti * TP
            tiles.append(dict(seqs=[s], n_seg=1, L=TP, x_src=xp[s][:, :, t0:t0 + TP], y_dst=yp[s][:, :, t0:t0 + TP],
                              first=ti == 0, last=ti == SEQ // TP - 1, prompt=True))
    if do_sample:
        tiles.append(dict(seqs=[2, 3, 4, 5], n_seg=4, L=64, x_src=xs, y_dst=ys, first=True, last=True, prompt=False))
    for n_, t_ in enumerate(tiles):
        nx = tiles[n_ + 1]["x_src"] if n_ + 1 < len(tiles) else None
        do_tile(x_loaded=n_ > 0, next_x=nx, **t_)
    S.barrier()


def _bf(x):
    return np.ascontiguousarray(x, dtype=np.float32)


def _consts():
    k = np.arange(64)
    cm = np.zeros((64, 1024), np.float32)
    cm[:, 0:64] = (k[:, None] <= k[None, :])
    cm[:, 64:128] = -(k[:, None] <= k[None, :]).astype(np.float32)
    cm[:, 128:192] = np.eye(64)
    cm[:, 192:256] = -(k[None, :] > k[:, None]).astype(np.float32)
    cm[:, 256:384] = 1.0
    neg = np.where(k[None, :] < k[:, None], NEGBIG, 0.0).astype(np.float32)
    cm[:, 384:896] = np.tile(neg, (1, 8))
    rc = np.zeros((128, 64), np.float32)
    for g, w in enumerate((2, 4, 8, 16)):
        rc[:, g * 16:(g + 1) * 16] = 1.0 / np.minimum(np.arange(16) + 1.0, float(w))
    return cm, rc


def _fm(v, nchunk):
    return np.asarray(v, np.float32).reshape(nchunk, 128).T


def make_in_maps(x_prompt, x_sample, c_prompt, c_sample, state_pool, state_conv, state_rec,
                 norm_gain, ada_w, ada_b, pool_w_in, pool_w_group, pool_scale, pool_w_out,
                 gdn_w_in, gdn_conv_w, gdn_a_log, gdn_dt_bias, gdn_norm_w, gdn_w_out, final_gain):
    cm, rc = _consts()
    vec = np.zeros((128, NV), np.float32)
    for l in range(2):
        vec[:, V_GAIN + l * 8:V_GAIN + (l + 1) * 8] = _fm(norm_gain[l], 8)
        vec[:, V_ADAB + l * 24:V_ADAB + (l + 1) * 24] = _fm(ada_b[l], 24)
    vec[:, V_PSCALE:V_PSCALE + 16] = _fm(pool_scale[0], 16)
    for tp in range(4):
        vec[:, V_CONVW + tp * 32:V_CONVW + (tp + 1) * 32] = _fm(gdn_conv_w[0, tp], 32)
    vec[:, V_NORMW] = np.asarray(gdn_norm_w[0], np.float32)
    vec[:, V_FGAIN:V_FGAIN + 8] = _fm(final_gain, 8)
    hv = np.zeros((64, 32), np.float32)
    hv[:, 0:16] = np.asarray(gdn_a_log[0], np.float32)[None, :]
    hv[:, 16:32] = np.asarray(gdn_dt_bias[0], np.float32)[None, :]
    shared = {
        "vec": vec, "hv": hv, "cm": cm, "rc": rc, "ident": np.eye(128, dtype=np.float32),
        "ada_w": _bf(ada_w), "pool_w_in": _bf(pool_w_in[0]), "pool_w_group": _bf(pool_w_group[0]),
        "pool_w_out": _bf(pool_w_out[0]), "gdn_w_in": _bf(gdn_w_in[0]), "gdn_w_out": _bf(gdn_w_out[0]),
    }
    x_prompt = np.asarray(x_prompt, np.float32)
    x_sample = np.asarray(x_sample, np.float32)
    in_maps = []
    for i in range(NCORES):
        xp = x_prompt[2 * i:2 * i + 2].reshape(2, SEQ, 8, 128).transpose(0, 3, 2, 1)
        xs = x_sample[4 * i:4 * i + 4].reshape(4, 64, 8, 128).transpose(3, 2, 0, 1).reshape(128, 8, 256)
        c6 = np.concatenate([np.asarray(c_prompt)[2 * i:2 * i + 2], np.asarray(c_sample)[4 * i:4 * i + 4]], 0)
        cT = c6.reshape(6, 8, 128).transpose(2, 1, 0)
        sp = np.asarray(state_pool)[0, 4 * i:4 * i + 4].reshape(4, 15, 16, 128).transpose(3, 2, 0, 1)
        sc = np.asarray(state_conv)[0, 4 * i:4 * i + 4].reshape(4, 3, 32, 128).transpose(3, 2, 0, 1)
        sr = np.asarray(state_rec)[0, 4 * i:4 * i + 4].transpose(0, 2, 1, 3)
        m = dict(shared)
        m.update({"xp": _bf(xp), "xs": _bf(xs), "cT": _bf(cT), "sp_in": _bf(sp), "sc_in": _bf(sc), "sr_in": _bf(sr)})
        in_maps.append(m)
    return in_maps


def assemble(results):
    y_prompt = np.zeros((16, SEQ, D), np.float32)
    y_sample = np.zeros((32, 64, D), np.float32)
    pool_p = np.zeros((1, 16, 15, 2048), np.float32)
    conv_p = np.zeros((1, 16, 3, 4096), np.float32)
    rec_p = np.zeros((1, 16, 16, 128, 128), np.float32)
    pool_s = np.zeros((1, 32, 15, 2048), np.float32)
    conv_s = np.zeros((1, 32, 3, 4096), np.float32)
    rec_s = np.zeros((1, 32, 16, 128, 128), np.float32)
    for i, r in enumerate(results):
        yp = np.asarray(r["yp"])
        y_prompt[2 * i:2 * i + 2] = yp.transpose(0, 3, 2, 1).reshape(2, SEQ, D)
        ys = np.asarray(r["ys"]).reshape(128, 8, 4, 64)
        y_sample[4 * i:4 * i + 4] = ys.transpose(2, 3, 1, 0).reshape(4, 64, D)
        po = np.asarray(r["pool_o"]).transpose(0, 3, 2, 1).reshape(6, 15, 2048)
        co = np.asarray(r["conv_o"]).transpose(0, 3, 2, 1).reshape(6, 3, 4096)
        ro = np.asarray(r["rec_o"]).transpose(0, 2, 1, 3)
        pool_p[0, 2 * i:2 * i + 2] = po[0:2]
        pool_s[0, 4 * i:4 * i + 4] = po[2:6]
        conv_p[0, 2 * i:2 * i + 2] = co[0:2]
        conv_s[0, 4 * i:4 * i + 4] = co[2:6]
        rec_p[0, 2 * i:2 * i + 2] = ro[0:2]
        rec_s[0, 4 * i:4 * i + 4] = ro[2:6]
    return (y_prompt, y_sample, pool_p, conv_p, rec_p, pool_s, conv_s, rec_s)


_NC_CACHE = {}


def kernel(**inputs):
    in_maps = make_in_maps(**inputs)
    if "nc" not in _NC_CACHE:
        _NC_CACHE["nc"] = build()
    res = run_bass_kernel_spmd(_NC_CACHE["nc"], in_maps, core_ids=list(range(NCORES)))
    return assemble(res.results)
```

```python
import numpy as np
from contextlib import ExitStack
import concourse.bass as bass
import concourse.mybir as mybir
from concourse.bass_utils import run_bass_kernel_spmd

F32 = mybir.dt.float32
BF16 = mybir.dt.bfloat16
AF = mybir.ActivationFunctionType
ALU = mybir.AluOpType
EPS = 1e-6
NCORES = 8
D = 1024
SEQ = 4096
TP = 256
NEGBIG = -30000.0
SAME_ENGINE_SYNC = True

PW_IN, WG, PW_OUT, GW_IN, GW_OUT = 0, 16, 20, 28, 52
NBLK = 60
V_GAIN, V_ADAB, V_PSCALE, V_CONVW, V_NORMW, V_FGAIN, NV = 0, 16, 64, 80, 208, 209, 217


class Tk:
    __slots__ = ("name", "w", "r", "x")

    def __init__(self, name, x=False):
        self.name = name
        self.w = None
        self.r = {}
        self.x = x


class Sched:
    def __init__(self, nc, es):
        self.nc = nc
        self.es = es
        self.eng = {"pe": nc.tensor, "act": nc.scalar, "dve": nc.vector, "pool": nc.gpsimd, "sp": nc.sync}
        self.sem = {}
        self.cnt = {}
        self.waited = {}
        self.dma_keys = []
        for k in self.eng:
            self.sem[k] = es.enter_context(nc.semaphore("s_" + k))
            self.cnt[k] = 0

    def dma_sem(self, key):
        if key not in self.sem:
            self.sem[key] = self.es.enter_context(self.nc.semaphore("d_" + key))
            self.cnt[key] = 0
            self.dma_keys.append(key)
        return key

    def _deps(self, R, W, eng=None):
        deps = {}

        def add(tok):
            if tok is None:
                return
            k, v = tok
            if deps.get(k, 0) < v:
                deps[k] = v

        for t in R:
            add(t.w)
            if t.x:
                for k, v in t.r.items():
                    if k != eng:
                        add((k, v))
        for t in W:
            add(t.w)
            for k, v in t.r.items():
                add((k, v))
        return deps

    def _wait(self, eng, deps):
        for k, v in deps.items():
            if k == eng and (eng == "pe" or eng == "sp" or not SAME_ENGINE_SYNC):
                continue
            if k in self.dma_keys:
                v = self.cnt[k]
            if self.waited.get((eng, k), 0) >= v:
                continue
            self.eng[eng].wait_ge(self.sem[k], v)
            self.waited[(eng, k)] = v

    def _mark(self, tok, R, W):
        k, v = tok
        for t in R:
            if t.r.get(k, 0) < v:
                t.r[k] = v
        for t in W:
            t.w = tok
            t.r = {}

    def op(self, eng, fn, R=(), W=()):
        self._wait(eng, self._deps(R, W, eng))
        ins = fn(self.eng[eng])
        self.cnt[eng] += 1
        ins.then_inc(self.sem[eng], 1)
        self._mark((eng, self.cnt[eng]), R, W)

    def dma(self, q, key, out, in_, R=(), W=(), **kw):
        key = self.dma_sem(key)
        self._wait(q, self._deps(R, W))
        ins = self.eng[q].dma_start(out=out, in_=in_, **kw)
        self.cnt[key] += 16
        ins.then_inc(self.sem[key], 16)
        self._mark((key, self.cnt[key]), R, W)

    def transfer(self, src, dst):
        for d in dst:
            for sr in src:
                toks = list(sr.r.items())
                if sr.w is not None:
                    toks.append(sr.w)
                for k, v in toks:
                    if d.r.get(k, 0) < v:
                        d.r[k] = v

    def barrier(self):
        for e in self.eng:
            for k in list(self.sem.keys()):
                if k == e or self.cnt[k] == 0:
                    continue
                if self.waited.get((e, k), 0) >= self.cnt[k]:
                    continue
                self.eng[e].wait_ge(self.sem[k], self.cnt[k])
                self.waited[(e, k)] = self.cnt[k]


def build(n_ptiles=SEQ // TP, do_sample=True):
    nc = bass.Bass("TRN2", target_bir_lowering=False)
    es = ExitStack()
    with es:
        _build(nc, es, n_ptiles, do_sample)
    return nc


def _build(nc, es, n_ptiles, do_sample):
    S = Sched(nc, es)

    def din(name, shape, dt=F32):
        return nc.dram_tensor(name, list(shape), dt, kind="ExternalInput").ap()

    def dout(name, shape, dt=F32):
        return nc.dram_tensor(name, list(shape), dt, kind="ExternalOutput").ap()

    xp = din("xp", [2, 128, 8, SEQ])
    xs = din("xs", [128, 8, 256])
    cT = din("cT", [128, 8, 6])
    sp_in = din("sp_in", [128, 16, 4, 15])
    sc_in = din("sc_in", [128, 32, 4, 3])
    sr_in = din("sr_in", [4, 128, 16, 128])
    vec_d = din("vec", [128, NV])
    hv_d = din("hv", [64, 32])
    cm_d = din("cm", [64, 1024])
    rc_d = din("rc", [128, 64])
    ada_w = din("ada_w", [2, D, 3 * D])
    pool_w_in = din("pool_w_in", [D, 4096])
    pool_w_group = din("pool_w_group", [4, 512, 512])
    pool_w_out = din("pool_w_out", [2048, D])
    gdn_w_in = din("gdn_w_in", [D, 6176])
    gdn_w_out = din("gdn_w_out", [2048, D])

    yp = dout("yp", [2, 128, 8, SEQ])
    ys = dout("ys", [128, 8, 256])
    pool_o = dout("pool_o", [6, 128, 16, 15])
    conv_o = dout("conv_o", [6, 128, 32, 3])
    rec_o = dout("rec_o", [6, 128, 16, 128])

    wsc = nc.dram_tensor("wsc", [NBLK, 128, 2048], BF16, kind="Internal").ap()
    wab_sc = nc.dram_tensor("wab_sc", [128, 8, 32], BF16, kind="Internal").ap()
    wsc_t = [Tk("wsc%d" % b) for b in range(NBLK)]
    wab_sc_t = Tk("wabsc")

    def sb(name, shape, dt=F32):
        return es.enter_context(nc.sbuf_tensor("sb_" + name, list(shape), dt))

    def blk_src(b):
        if b < WG:
            return pool_w_in[:, b * 256:(b + 1) * 256].rearrange("(k p) c -> p k c", p=128), 8, 256
        if b < PW_OUT:
            return pool_w_group[b - WG].rearrange("(k p) c -> p k c", p=128), 4, 512
        if b < GW_IN:
            j = b - PW_OUT
            return pool_w_out[:, j * 128:(j + 1) * 128].rearrange("(k p) c -> p k c", p=128), 16, 128
        if b < GW_OUT:
            j = b - GW_IN
            return gdn_w_in[:, j * 256:(j + 1) * 256].rearrange("(k p) c -> p k c", p=128), 8, 256
        j = b - GW_OUT
        return gdn_w_out[:, j * 128:(j + 1) * 128].rearrange("(k p) c -> p k c", p=128), 16, 128

    GORD = (3, 2, 1, 0)
    l0_blocks = [PW_IN + 8 + 2 * GORD[0], PW_IN + 9 + 2 * GORD[0], PW_IN + 2 * GORD[0], PW_IN + 2 * GORD[0] + 1]
    for gi in range(4):
        if gi < 3:
            gn = GORD[gi + 1]
            l0_blocks += [PW_IN + 8 + 2 * gn, PW_IN + 9 + 2 * gn, PW_IN + 2 * gn, PW_IN + 2 * gn + 1]
        l0_blocks += [WG + GORD[gi]]
    per_tile_blocks = (l0_blocks + [PW_OUT + i for i in range(8)] + [GW_IN + i for i in range(24)] +
                       [GW_OUT + i for i in range(8)])
    assert sorted(per_tile_blocks) == list(range(NBLK))

    S.dma("pool", "castw", wab_sc, gdn_w_in[:, 6144:6176].rearrange("(k p) c -> p k c", p=128), R=(), W=[wab_sc_t])
    blk_shape = {}
    NCASTQ = 6
    for n_, b in enumerate(per_tile_blocks):
        src, kc, cols = blk_src(b)
        blk_shape[b] = (kc, cols)
        dst = wsc[b][:, 0:kc * cols].rearrange("p (k c) -> p k c", k=kc)
        key = "cast%d" % (n_ % NCASTQ)
        if n_ >= NCASTQ:
            S._wait("pool", {key: S.cnt[key]})
        S.dma("pool", key, dst, src, R=(), W=[wsc_t[b]])

    vec = sb("vec", [128, NV]); vec_t = Tk("vec")
    hv = sb("hv", [64, 32]); hv_t = Tk("hv")
    cm = sb("cm", [64, 1024]); cm_t = Tk("cm")
    rc = sb("rc", [128, 64]); rc_t = Tk("rc")
    cts = sb("cts", [128, 8, 6]); cts_t = Tk("cts")
    wab = sb("wab", [128, 8, 32], BF16); wab_t = Tk("wab")
    S.dma("sp", "const", vec[:, :], vec_d, W=[vec_t])
    S.dma("sp", "const", hv[:, :], hv_d, W=[hv_t])
    S.dma("sp", "const", cm[:, :], cm_d, W=[cm_t])
    S.dma("sp", "const", rc[:, :], rc_d, W=[rc_t])
    S.dma("sp", "const", cts[:, :, :], cT, W=[cts_t])
    Utri = cm[:, 0:64]
    negU = cm[:, 64:128]
    ident_f = cm[:, 128:192]
    negstrict = cm[:, 192:256]
    ones64 = cm[:, 256:384]
    NEGrep = cm[:, 384:896]
    dtb = hv[:, 16:32]

    ones_mean = sb("ones_mean", [128, 128], BF16); ones_mean_t = Tk("om")
    ones_one = sb("ones_one", [128, 128], BF16); ones_one_t = Tk("oo")
    ones_hd = sb("ones_hd", [128, 128], BF16); ones_hd_t = Tk("oh")
    ident_b = sb("ident_b", [128, 128], BF16); ident_b_t = Tk("ib")
    ident_s = sb("ident_s", [128, 128]); ident_s_t = Tk("is")
    ident_d = din("ident", [128, 128])
    S.op("pool", lambda e: e.memset(ones_mean[:, :], 1.0 / 1024.0), W=[ones_mean_t])
    S.op("pool", lambda e: e.memset(ones_one[:, :], 1.0), W=[ones_one_t])
    S.op("pool", lambda e: e.memset(ones_hd[:, :], 1.0 / 128.0), W=[ones_hd_t])
    S.dma("sp", "const", ident_s[:, :], ident_d, W=[ident_s_t])
    S.op("dve", lambda e: e.tensor_copy(out=ident_b[:, :], in_=ident_s[:, :]), R=[ident_s_t], W=[ident_b_t])
    negA = sb("negA", [64, 16]); negA_t = Tk("negA")
    S.op("act", lambda e: e.activation(out=negA[:, :], in_=hv[:, 0:16], func=AF.Exp), R=[hv_t], W=[negA_t])
    S.op("dve", lambda e: e.tensor_scalar(out=negA[:, :], in0=negA[:, :], scalar1=-1.0, scalar2=None, op0=ALU.mult),
         R=[negA_t], W=[negA_t])

    ps = es.enter_context(nc.psum_tensor("ps", [128, 8, 512], F32))
    bank_t = [Tk("bank%d" % i, x=True) for i in range(8)]
    bank_i = [0]

    class Bank:
        def __init__(self, i):
            self.i = i
            self.t = bank_t[i]
            self.ap = ps[:, i, :]
            self.bf = ps[:, i, :].bitcast(BF16)

    free_banks = list(range(8))

    def nextbank():
        assert free_banks, "out of PSUM banks"
        return Bank(free_banks.pop(0))

    def rel(b):
        assert b.i not in free_banks
        free_banks.append(b.i)

    def MM(out, lhsT, rhs, start, stop, R, W):
        S.op("pe", lambda e: e.matmul(out, lhsT=lhsT, rhs=rhs, start=start, stop=stop), R=R, W=W)

    def ACT(out, in_, func, R, W, bias=0.0, scale=1.0):
        S.op("act", lambda e: e.activation(out=out, in_=in_, func=func, bias=bias, scale=scale), R=R, W=W)

    def TT(eng, out, in0, in1, op, R, W):
        S.op(eng, lambda e: e.tensor_tensor(out=out, in0=in0, in1=in1, op=op), R=R, W=W)

    def TS(eng, out, in0, s1, s2, op0, op1, R, W):
        if s2 is None:
            S.op(eng, lambda e: e.tensor_scalar(out=out, in0=in0, scalar1=s1, scalar2=None, op0=op0), R=R, W=W)
        else:
            S.op(eng, lambda e: e.tensor_scalar(out=out, in0=in0, scalar1=s1, scalar2=s2, op0=op0, op1=op1), R=R, W=W)

    def STT(eng, out, in0, scalar, in1, op0, op1, R, W):
        S.op(eng, lambda e: e.scalar_tensor_tensor(out=out, in0=in0, scalar=scalar, in1=in1, op0=op0, op1=op1),
             R=R, W=W)

    kc_ = sb("kconst", [128, 4]); kc_t = Tk("kconst")
    S.op("pool", lambda e: e.memset(kc_[:, 0:1], EPS), W=[kc_t])
    S.op("pool", lambda e: e.memset(kc_[:, 1:2], -0.5), W=[kc_t])
    S.op("pool", lambda e: e.memset(kc_[:, 2:3], -1.0), W=[kc_t])
    S.op("pool", lambda e: e.memset(kc_[:, 3:4], 0.0), W=[kc_t])
    S.op("pool", lambda e: e.memset(kc_[0:8, 3:4], float(np.log(128.0 ** -0.5))), W=[kc_t])
    epsb = kc_[:, 0:1]

    def POW(out, cidx, R, W):
        npart = out.shape[0]
        ex = kc_[0:npart, cidx:cidx + 1]
        if len(out.shape) == 3:
            ex = ex.unsqueeze(2)
        TT("pool", out, out, ex.broadcast_to(list(out.shape)), ALU.pow, R=list(R) + [kc_t], W=W)

    def RSQ(out, in_, R, W):
        npart = out.shape[0]
        ACT(out, in_, AF.Ln, R=list(R) + [kc_t], W=W, bias=epsb[0:npart, :])
        ACT(out, out, AF.Exp, R=W, W=W, scale=-0.5)

    def CP(eng, out, in_, R, W):
        S.op(eng, lambda e: e.tensor_copy(out=out, in_=in_), R=R, W=W)

    def run_il(gens, depth):
        gens = iter(gens)
        active = []
        while True:
            while len(active) < depth:
                g = next(gens, None)
                if g is None:
                    break
                active.append(g)
            if not active:
                return
            for g in list(active):
                try:
                    next(g)
                except StopIteration:
                    active.remove(g)

    def run_pair(a, b, ratio):
        a_done = a is None
        b_done = b is None
        while not (a_done and b_done):
            if not a_done:
                try:
                    next(a)
                except StopIteration:
                    a_done = True
            for _ in range(ratio if not a_done else 1000000):
                if b_done:
                    break
                try:
                    next(b)
                except StopIteration:
                    b_done = True

    def chain_gens(gens):
        for g in gens:
            for _ in g:
                yield

    mod = sb("mod", [128, 2, 24, 6]); mod_t = Tk("mod")
    gs = sb("gs", [128, 2, 8, 6]); gs_t = Tk("gs")
    with ExitStack() as es2:
        aw = [es2.enter_context(nc.sbuf_tensor("aw%d" % i, [128, 8, 512], F32)) for i in range(2)]
        aw_t = [Tk("aw0"), Tk("aw1")]
        ACT(cts[:, :, :], cts[:, :, :], AF.Silu, R=[cts_t], W=[cts_t])
        for l in range(2):
            bk = nextbank()
            for j in range(6):
                sl = (l * 6 + j) % 2
                S.dma("sp", "aw%d" % sl, aw[sl][:, :, :],
                      ada_w[l][:, j * 512:(j + 1) * 512].rearrange("(k p) c -> p k c", p=128), W=[aw_t[sl]])
                for ff in range(4):
                    f = j * 4 + ff
                    for k in range(8):
                        MM(bk.ap[:, f * 6:(f + 1) * 6], aw[sl][:, k, ff * 128:(ff + 1) * 128], cts[:, k, :],
                           k == 0, k == 7, R=[aw_t[sl], cts_t], W=[bk.t])
            TT("dve", mod[:, l, :, :], bk.ap[:, 0:144].rearrange("p (f s) -> p f s", s=6),
               vec[:, V_ADAB + l * 24:V_ADAB + (l + 1) * 24].unsqueeze(2).broadcast_to([128, 24, 6]), ALU.add,
               R=[bk.t, vec_t], W=[mod_t])
            rel(bk)
            STT("dve", gs[:, l, :, :], mod[:, l, 8:16, :], 1.0,
                vec[:, V_GAIN + l * 8:V_GAIN + (l + 1) * 8].unsqueeze(2).broadcast_to([128, 8, 6]),
                ALU.add, ALU.mult, R=[mod_t, vec_t], W=[gs_t])
        S.barrier()

    S.dma("sp", "const", wab[:, :, :], wab_sc, R=[wab_sc_t], W=[wab_t])

    def shift_ap(l, k, sid):
        return mod[:, l, k, sid:sid + 1]

    def gate_ap(l, k, sid):
        return mod[:, l, 16 + k, sid:sid + 1]

    def gs_ap(l, k, sid):
        return gs[:, l, k, sid:sid + 1]

    NPIPE = 3
    xt = sb("xt", [128, 8, TP]); x_t = [Tk("x%d" % k) for k in range(8)]
    hT = sb("hT", [128, 8, TP], BF16); h_t = [Tk("h%d" % k) for k in range(8)]
    sz = sb("sz", [128, 16, TP], BF16); sz_t = [Tk("sz%d" % k) for k in range(16)]
    rstd = sb("rstd", [128, TP]); rstd_t = Tk("rstd")
    sq = [sb("sq%d" % i, [128, TP], BF16) for i in range(NPIPE)]; sq_t = [Tk("sq%d" % i) for i in range(NPIPE)]
    tmpf = [sb("tmpf%d" % i, [128, TP]) for i in range(NPIPE)]; tmpf_t = [Tk("tf%d" % i) for i in range(NPIPE)]
    NSLOT = 4
    wslot = [sb("wslot%d" % i, [128, 2048], BF16) for i in range(NSLOT)]
    wslot_t = [Tk("ws%d" % i) for i in range(NSLOT)]
    uhist = sb("uhist", [128, 16, 4, 15]); uhist_t = [Tk("uh%d" % g) for g in range(4)]
    chist = sb("chist", [128, 32, 4, 3]); chist_t = [Tk("ch%d" % f) for f in range(32)]
    qT = sb("qT", [128, 8, TP], BF16); q_t = [Tk("q%d" % k) for k in range(8)]
    kT = sb("kT", [128, 8, TP], BF16); k_t = [Tk("k%d" % k) for k in range(8)]
    vT = sb("vT", [128, 16, TP], BF16); v_t = [Tk("v%d" % k) for k in range(16)]
    UB = max(4 * (15 + TP), 4 * 4 * (15 + 64))
    AW = 2 * UB
    assert AW >= 2048
    arA = sb("arA", [128, AW])
    arB = sb("arB", [128, 2048])
    arC = sb("arC", [128, 2048])
    arD = sb("arD", [128, 4 * TP], BF16)
    arE = sb("arE", [128, 2048]); arE_t = Tk("arE")
    ubuf = [arA[:, 0:UB], arA[:, UB:2 * UB]]; ubuf_t = [Tk("ubuf0"), Tk("ubuf1")]
    tmpA = arB; tmpA_t = Tk("tmpA")
    tmpB = arC; tmpB_t = Tk("tmpB")
    dT = arD[:, 0:4 * TP].rearrange("p (a t) -> p a t", a=4); dT_t = Tk("dT")
    raw = [sb("raw%d" % i, [128, 3 + TP + 13]) for i in range(NPIPE)]; raw_t = [Tk("raw%d" % i) for i in range(NPIPE)]
    acc = [sb("acc%d" % i, [128, TP]) for i in range(NPIPE)]; acc_t = [Tk("acc%d" % i) for i in range(NPIPE)]
    ones_sel = sb("ones_sel", [128, 16, 16], BF16); ones_sel_t = Tk("osel")
    S.op("pool", lambda e: e.memset(ones_sel[:, :, :], 0.0), W=[ones_sel_t])
    for f_ in range(16):
        S.op("pool", lambda e: e.memset(ones_sel[:, f_, f_:f_ + 1], 1.0), W=[ones_sel_t])
    rinv16 = sb("rinv16", [16, TP]); rinv16_t = Tk("rinv16")
    rhi = sb("rhi", [16, TP], BF16); rhi_t = Tk("rhi")
    rlo = sb("rlo", [16, TP], BF16); rlo_t = Tk("rlo")
    sel16 = sb("sel16", [16, 16, 128], BF16); sel16_t = Tk("sel16")
    S.op("dve", lambda e: e.tensor_copy(out=sel16[:, :, :],
                                        in_=ident_s[0:16, 0:16].unsqueeze(2).broadcast_to([16, 16, 128])),
         R=[ident_s_t], W=[sel16_t])
    Sst = sb("Sst", [128, 16, 128]); Sst_t = Tk("S")
    Sbf = [sb("Sbf%d" % i, [128, 16, 128], BF16) for i in range(2)]; Sbf_t = [Tk("Sbf0"), Tk("Sbf1")]
    sm = [sb("sm%d" % i, [128, 8, 16]) for i in range(2)]
    sm_t = [[Tk("sm%d_%d" % (i, j)) for j in range(8)] for i in range(2)]
    FX = arA[0:64, 0:1024].rearrange("p (a c) -> p a c", a=16); FX_t = Tk("FX")
    FG = arA[0:64, 1024:2048].rearrange("p (a c) -> p a c", a=16); FG_t = Tk("FG")
    FD = arB[0:64, 0:1024].rearrange("p (a c) -> p a c", a=16); FD_t = Tk("FD")
    FN = arB[0:64, 1024:2048].rearrange("p (a c) -> p a c", a=16); FN_t = Tk("FN")
    F1 = arC[0:64, 0:1024].rearrange("p (a c) -> p a c", a=16); F1_t = Tk("F1")
    FE = arC[:, 1024:2048].rearrange("p (a c) -> p a c", a=16); FE_t = Tk("FE")
    NP2 = [sb("NP%d" % i, [64, 16, 2, 64], F32) for i in range(2)]
    Nb2 = [sb("Nb%d" % i, [64, 16, 64], F32) for i in range(2)]
    NT_t2 = [[Tk("NTa%d" % i), Tk("NTb%d" % i)] for i in range(2)]
    P_t2 = [[Tk("Pa%d" % i), Tk("Pb%d" % i)] for i in range(2)]
    N_t2 = [[Tk("Na%d" % i), Tk("Nb%d" % i)] for i in range(2)]
    TTb = [sb("TTb%d" % i, [64, 16, 64], BF16) for i in range(2)]; TTb_t = [Tk("TTb0"), Tk("TTb1")]
    ATb = [sb("AT%d" % i, [64, 16, 64], BF16) for i in range(2)]; AT_t = [Tk("AT0"), Tk("AT1")]
    qd = [sb("qd%d" % i, [128, 16, 64], BF16) for i in range(2)]; qd_t = [Tk("qd0"), Tk("qd1")]
    Ktok = sb("Ktok", [64, 8, 128], BF16); Ktok_t = Tk("Ktok")
    Vtok = sb("Vtok", [64, 16, 128], BF16); Vtok_t = Tk("Vtok")
    Wn = arE[0:64, :].rearrange("p (a e) -> p a e", a=16); Wn_t = arE_t
    rso = arE[:, 0:1024].rearrange("p (a c) -> p a c", a=16); rso_t = arE_t
    og = arE[:, 1024:2048].rearrange("p (a c) -> p a c", a=16); og_t = arE_t
    Wp = sb("Wp", [64, 16, 128], BF16); Wp_t = Tk("Wp")
    vnew = sb("vnew", [64, 16, 128], BF16); vnew_t = Tk("vnew")
    vnew2 = sb("vnew2", [64, 16, 128], BF16); vnew2_t = Tk("vnew2")
    osq = sb("osq", [128, 16, 64], BF16); osq_t = Tk("osq")

    n_tiles_total = 2 * n_ptiles + (1 if do_sample else 0)
    wseq = per_tile_blocks * n_tiles_total
    wstate = {"use": 0, "load": 0, "cache": {}}

    def w_issue_loads(upto):
        while wstate["load"] < min(upto, len(wseq)):
            i = wstate["load"]
            b = wseq[i]
            kc, cols = blk_shape[b]
            sl = i % NSLOT
            S.dma("sp", "w%d" % sl, wslot[sl][:, 0:kc * cols], wsc[b][:, 0:kc * cols], R=[wsc_t[b]], W=[wslot_t[sl]])
            wstate["load"] += 1

    def get_blk(b):
        c = wstate["cache"]
        if b in c:
            return c[b]
        i = wstate["use"]
        assert wseq[i] == b, (i, wseq[i], b)
        w_issue_loads(i + NSLOT)
        kc, cols = blk_shape[b]
        sl = i % NSLOT
        wstate["use"] += 1
        c.clear()
        c[b] = (wslot[sl][:, 0:kc * cols].rearrange("p (k c) -> p k c", k=kc), wslot_t[sl])
        return c[b]

    def rms_rstd(T):
        bk = nextbank()
        for k in range(8):
            i = k % NPIPE
            ACT(sq[i][:, :T], xt[:, k, :T], AF.Square, R=[x_t[k]], W=[sq_t[i]])
            MM(bk.ap[:, :T], ones_mean[:, :], sq[i][:, :T], k == 0, k == 7, R=[ones_mean_t, sq_t[i]], W=[bk.t])
        RSQ(rstd[:, :T], bk.ap[:, :T], R=[bk.t], W=[rstd_t])
        rel(bk)

    def make_h(l, T, segs):
        rms_rstd(T)
        for k in range(8):
            i = k % NPIPE
            tf = tmpf[i]
            TT("dve", tf[:, :T], xt[:, k, :T], rstd[:, :T], ALU.mult, R=[x_t[k], rstd_t], W=[tmpf_t[i]])
            for (c0, c1, sid) in segs:
                ACT(hT[:, k, c0:c1], tf[:, c0:c1], AF.Identity, R=[tmpf_t[i], gs_t, mod_t], W=[h_t[k]],
                    bias=shift_ap(l, k, sid), scale=gs_ap(l, k, sid))

    def proj(wv, wt, col0, T):
        bk = nextbank()
        for k in range(8):
            MM(bk.ap[:, :T], wv[:, k, col0:col0 + 128], hT[:, k, :T], k == 0, k == 7, R=[wt, h_t[k]], W=[bk.t])
        return bk

    def zproj_gen(blk, col0, fz, T):
        wv, wt = get_blk(blk)
        bk = proj(wv, wt, col0, T)
        yield
        ACT(sz[:, fz, :T], bk.ap[:, :T], AF.Silu, R=[bk.t], W=[sz_t[fz]])
        rel(bk)
        yield

    def outp_gen(l, blk, fo, T, segs):
        wv, wt = get_blk(blk)
        bk = nextbank()
        for k in range(16):
            MM(bk.ap[:, :T], wv[:, k, :], sz[:, k, :T], k == 0, k == 15, R=[wt, sz_t[k]], W=[bk.t])
        yield
        for (c0, c1, sid) in segs:
            STT("dve", xt[:, fo, c0:c1], bk.ap[:, c0:c1], gate_ap(l, fo, sid), xt[:, fo, c0:c1],
                ALU.mult, ALU.add, R=[bk.t, mod_t, x_t[fo]], W=[x_t[fo]])
        rel(bk)
        yield

    sb_cur = [0]

    def do_tile(seqs, n_seg, L, x_src, y_dst, first, last, prompt, x_loaded=False, next_x=None):
        T = n_seg * L
        C = T // 64
        segs = [(s * L, (s + 1) * L, seqs[s]) for s in range(n_seg)]
        wstate["cache"].clear()
        if not x_loaded:
            for k in range(8):
                S.dma("sp", "xld", xt[:, k, :T], x_src[:, k, :], W=[x_t[k]])
        if first:
            if prompt:
                for g in range(4):
                    S.op("pool", lambda e: e.memset(uhist[:, 4 * g:4 * g + 4, :, :], 0.0), W=[uhist_t[g]])
                S.op("pool", lambda e: e.memset(chist[:, :, :, :], 0.0), W=chist_t)
                S.op("pool", lambda e: e.memset(Sst[:, :, :], 0.0), W=[Sst_t])
                S.op("pool", lambda e: e.memset(Sbf[sb_cur[0]][:, :, :], 0.0), W=[Sbf_t[sb_cur[0]]])
            else:
                S.dma("pool", "hld", uhist[:, :, :, :], sp_in, W=uhist_t)
                S.dma("pool", "hld", chist[:, :, :, :], sc_in, W=chist_t)
        S.transfer([FX_t, FG_t], ubuf_t)
        S.transfer([FD_t, FN_t], [tmpA_t])
        S.transfer([F1_t, FE_t], [tmpB_t])

        make_h(0, T, segs)
        EL = 15 + L

        def v4(ap):
            return ap[:, 0:4 * n_seg * EL].rearrange("p (a s t) -> p a s t", a=4, s=n_seg)

        ub4 = [v4(ubuf[0]), v4(ubuf[1])]
        tA4 = v4(tmpA)
        tB4 = v4(tmpB)

        def z0(j4):
            gens = []
            for ff in range(4):
                fz = j4 * 4 + ff
                gens.append(zproj_gen(PW_IN + 8 + fz // 2, (fz % 2) * 128, fz, T))
            run_il(gens, 4)

        def s1(g, ui):
            u = ub4[ui]
            ut = ubuf_t[ui]
            CP("pool", u[:, :, :, 0:15], uhist[:, 4 * g:4 * g + 4, 0:n_seg, :], R=[uhist_t[g]], W=[ut])
            banks = []
            for fu in range(4):
                ch = 4 * g + fu
                wv, wt = get_blk(PW_IN + ch // 2)
                banks.append(proj(wv, wt, (ch % 2) * 128, T))
            for fu in range(4):
                bk = banks[fu]
                ACT(u[:, fu, :, 15:EL], bk.ap[:, :T].rearrange("p (s t) -> p s t", s=n_seg), AF.Copy,
                    R=[bk.t], W=[ut])
                rel(bk)
            CP("pool", uhist[:, 4 * g:4 * g + 4, 0:n_seg, :], u[:, :, :, L:EL], R=[ut], W=[uhist_t[g]])

        def s3(g, ui):
            w = (2, 4, 8, 16)[g]
            u = ub4[ui]
            ut = ubuf_t[ui]
            src, src_t = u, ut
            dsts = [(tA4, tmpA_t), (tB4, tmpB_t)]
            sh = 1
            di = 0
            while sh < w:
                dst, dst_t = dsts[di]
                lo = 2 * sh - 1
                TT("dve", dst[:, :, :, lo:EL], src[:, :, :, lo:EL], src[:, :, :, lo - sh:EL - sh],
                   ALU.add, R=[src_t], W=[dst_t])
                src, src_t = dst, dst_t
                di = 1 - di
                sh *= 2
            d4 = dT[:, :, :T].rearrange("p a (s t) -> p a s t", s=n_seg)
            STT("dve", d4, src[:, :, :, 15:EL], 1.0 / w, u[:, :, :, 15:EL], ALU.mult, ALU.subtract,
                R=[src_t, ut], W=[dT_t])
            if first and prompt:
                nfix = w - 1
                tfx = tmpf[0][:, 0:4 * nfix].rearrange("p (a t) -> p a t", a=4)
                TT("dve", tfx, src[:, :, 0, 15:15 + nfix],
                   rc[:, g * 16:g * 16 + nfix].unsqueeze(1).broadcast_to([128, 4, nfix]), ALU.mult,
                   R=[src_t, rc_t], W=[tmpf_t[0]])
                TT("dve", dT[:, :, 0:nfix], tfx, u[:, :, 0, 15:15 + nfix], ALU.subtract,
                   R=[tmpf_t[0], ut], W=[dT_t])

        def s4(g):
            wgv, wgt = get_blk(WG + g)
            banks = []
            for fo in range(4):
                bk = nextbank()
                banks.append(bk)
                for kk in range(4):
                    MM(bk.ap[:, :T], wgv[:, kk, fo * 128:(fo + 1) * 128], dT[:, kk, :T], kk == 0, kk == 3,
                       R=[wgt, dT_t], W=[bk.t])
            for fo in range(4):
                ch = 4 * g + fo
                bk = banks[fo]
                STT("dve", sz[:, ch, :T], bk.ap[:, :T], vec[:, V_PSCALE + ch:V_PSCALE + ch + 1], sz[:, ch, :T],
                    ALU.mult, ALU.mult, R=[bk.t, vec_t, sz_t[ch]], W=[sz_t[ch]])
                rel(bk)

        z0(GORD[0])
        s1(GORD[0], 0)
        for gi in range(4):
            if gi < 3:
                z0(GORD[gi + 1])
            s3(GORD[gi], gi % 2)
            if gi < 3:
                s1(GORD[gi + 1], (gi + 1) % 2)
            s4(GORD[gi])
        if last:
            for (c0, c1, sid) in segs:
                s = c0 // L
                S.dma("pool", "ost", pool_o[sid], uhist[:, :, s, :], R=uhist_t, W=[])
        run_il([outp_gen(0, PW_OUT + fo, fo, T, segs) for fo in range(8)], 4)

        make_h(1, T, segs)

        pipe_free = list(range(NPIPE))
        ssb = nextbank()

        def qkv_gen(f):
            i = pipe_free.pop(0)
            wv, wt = get_blk(GW_IN + f // 2)
            bk = proj(wv, wt, (f % 2) * 128, T)
            yield
            rw, rwt = raw[i], raw_t[i]
            ac, act_ = acc[i], acc_t[i]
            r3 = rw[:, 0:n_seg * (3 + L)].rearrange("p (s t) -> p s t", s=n_seg)
            a3 = ac[:, :T].rearrange("p (s t) -> p s t", s=n_seg)
            b3 = bk.ap[:, :T].rearrange("p (s t) -> p s t", s=n_seg)
            CP("pool", r3[:, :, 0:3], chist[:, f, 0:n_seg, :], R=[chist_t[f]], W=[rwt])
            ACT(r3[:, :, 3:3 + L], b3, AF.Copy, R=[bk.t], W=[rwt])
            ACT(a3, b3, AF.Identity, R=[bk.t, vec_t], W=[act_],
                scale=vec[:, V_CONVW + 3 * 32 + f:V_CONVW + 3 * 32 + f + 1])
            rel(bk)
            yield
            CP("pool", chist[:, f, 0:n_seg, :], r3[:, :, L:L + 3], R=[rwt], W=[chist_t[f]])
            for tp in range(3):
                cw = vec[:, V_CONVW + tp * 32 + f:V_CONVW + tp * 32 + f + 1]
                STT("dve", a3, r3[:, :, tp:tp + L], cw, a3, ALU.mult, ALU.add, R=[rwt, vec_t, act_], W=[act_])
                yield
            if f < 16:
                ACT(ac[:, :T], ac[:, :T], AF.Silu, R=[act_], W=[act_])
                yield
                ACT(sq[i][:, :T], ac[:, :T], AF.Square, R=[act_], W=[sq_t[i]])
                MM(ssb.ap[0:16, :T], ones_sel[:, f, :], sq[i][:, :T], f == 0, f == 15, R=[ones_sel_t, sq_t[i]], W=[ssb.t])
                dst, dst_t = (qT[:, f, :T], q_t[f]) if f < 8 else (kT[:, f - 8, :T], k_t[f - 8])
                CP("dve", dst, ac[:, :T], R=[act_], W=[dst_t])
            else:
                ACT(vT[:, f - 16, :T], ac[:, :T], AF.Silu, R=[act_], W=[v_t[f - 16]])
            pipe_free.append(i)
            yield

        z1_gens = [zproj_gen(GW_IN + 16 + fz // 2, (fz % 2) * 128, fz, T) for fz in range(16)]
        run_il([qkv_gen(f) for f in range(32)] + z1_gens, NPIPE)
        ACT(rinv16[:, :T], ssb.ap[0:16, :T], AF.Ln, R=[ssb.t, kc_t], W=[rinv16_t], bias=epsb[0:16, :])
        rel(ssb)
        ACT(rinv16[:, :T], rinv16[:, :T], AF.Exp, R=[rinv16_t, kc_t], W=[rinv16_t], scale=-0.5, bias=kc_[0:16, 3:4])
        CP("dve", rhi[:, :T], rinv16[:, :T], R=[rinv16_t], W=[rhi_t])
        TT("dve", rlo[:, :T], rinv16[:, :T], rhi[:, :T], ALU.subtract, R=[rinv16_t, rhi_t], W=[rlo_t])
        def l2n_gen():
            for f in range(16):
                bq = nextbank()
                oh = sel16[:, f, :]
                MM(bq.ap[:, :T], oh, rhi[:, :T], True, False, R=[sel16_t, rhi_t], W=[bq.t])
                MM(bq.ap[:, :T], oh, rlo[:, :T], False, True, R=[sel16_t, rlo_t], W=[bq.t])
                dst, dst_t = (qT[:, f, :T], q_t[f]) if f < 8 else (kT[:, f - 8, :T], k_t[f - 8])
                TT("dve", dst, dst, bq.ap[:, :T], ALU.mult, R=[dst_t, bq.t], W=[dst_t])
                rel(bq)
                yield
        if last:
            for (c0, c1, sid) in segs:
                s = c0 // L
                S.dma("pool", "ost", conv_o[sid], chist[:, :, s, :], R=chist_t, W=[])
        S.transfer(ubuf_t, [FX_t, FG_t])
        S.transfer([tmpA_t], [FD_t, FN_t])
        S.transfer([tmpB_t], [F1_t, FE_t])

        def prep(c):
            for _ in front(c):
                yield
            for _ in solve(c):
                yield

        def solve(c):
            p = c % 2
            NP, Nb, NT_t, P_t, N_t = NP2[p], Nb2[p], NT_t2[p], P_t2[p], N_t2[p]
            yield
            for lvl in range(5):
                k1 = lvl == 0
                k16 = lvl == 4
                for half in range(2):
                    hs = slice(half * 8, half * 8 + 8)
                    Rm = [N_t[half], NT_t[half], P_t[half]]
                    a_banks = []
                    if k1 or k16:
                        slot = 0 if k1 else 1
                        bA = nextbank()
                        a_banks.append(bA)
                        for hh in range(8):
                            h = half * 8 + hh
                            MM(bA.ap[0:64, hh * 64:(hh + 1) * 64], Nb[:, h, :], NP[:, h, slot, :], True, True,
                               R=Rm, W=[bA.t])
                    else:
                        for q in range(2):
                            bA = nextbank()
                            a_banks.append(bA)
                            for hh in range(4):
                                h = half * 8 + q * 4 + hh
                                MM(bA.ap[0:64, hh * 128:(hh + 1) * 128], Nb[:, h, :],
                                   NP[:, h, :, :].rearrange("p s c -> p (s c)"), True, True, R=Rm, W=[bA.t])
                    bB = nextbank()
                    for hh in range(8):
                        h = half * 8 + hh
                        MM(bB.ap[0:64, hh * 64:(hh + 1) * 64], NP[:, h, 0, :], Nb[:, h, :], True, True,
                           R=Rm, W=[bB.t])
                    yield
                    if k1:
                        ACT(NP[:, hs, 0, :], a_banks[0].ap[0:64, :].rearrange("p (a c) -> p a c", a=8), AF.Copy,
                            R=[a_banks[0].t], W=[NT_t[half]])
                    elif k16:
                        TT("dve", NP[:, hs, 1, :], a_banks[0].ap[0:64, :].rearrange("p (a c) -> p a c", a=8),
                           NP[:, hs, 1, :], ALU.add, R=[a_banks[0].t, P_t[half]], W=[P_t[half]])
                    else:
                        for q in range(2):
                            h4 = slice(half * 8 + q * 4, half * 8 + q * 4 + 4)
                            b4 = a_banks[q].ap[0:64, :].rearrange("p (a s c) -> p a s c", a=4, s=2)
                            TT("dve", NP[:, h4, 1, :], b4[:, :, 1, :], NP[:, h4, 1, :], ALU.add,
                               R=[a_banks[q].t, P_t[half]], W=[P_t[half]])
                            ACT(NP[:, h4, 0, :], b4[:, :, 0, :], AF.Copy, R=[a_banks[q].t, P_t[half]], W=[NT_t[half]])
                    for bA in a_banks:
                        rel(bA)
                    yield
                    ACT(Nb[:, hs, :], bB.ap[0:64, :].rearrange("p (a c) -> p a c", a=8), AF.Copy,
                        R=[bB.t], W=[N_t[half]])
                    rel(bB)
                    yield
            for half in range(2):
                hs = slice(half * 8, half * 8 + 8)
                bP = nextbank()
                for hh in range(8):
                    h = half * 8 + hh
                    MM(bP.ap[0:64, hh * 64:(hh + 1) * 64], Nb[:, h, :], NP[:, h, 1, :], True, True,
                       R=[N_t[half], P_t[half]], W=[bP.t])
                yield
                TT("dve", TTb[p][:, hs, :], bP.ap[0:64, :].rearrange("p (a c) -> p a c", a=8), NP[:, hs, 1, :],
                   ALU.add, R=[bP.t, P_t[half]], W=[TTb_t[p]])
                rel(bP)
                yield

        def front(c):
            p = c % 2
            NP, Nb, NT_t, P_t, N_t = NP2[p], Nb2[p], NT_t2[p], P_t2[p], N_t2[p]
            tok = slice(c * 64, (c + 1) * 64)
            smp, st = sm[p], sm_t[p]
            g_ = smp[0:64, 0, :]; beta = smp[0:64, 1, :]; gcs = smp[0:64, 2, :]; egc = smp[0:64, 3, :]
            negegc = smp[0:64, 4, :]; ekd = smp[0:64, 5, :]; bekd = smp[0:64, 6, :]; egl = smp[:, 7, :]
            bk = nextbank()
            for k in range(8):
                MM(bk.ap[0:64, 0:32], hT[:, k, tok], wab[:, k, :], k == 0, k == 7, R=[h_t[k], wab_t], W=[bk.t])
            yield
            TT("dve", g_, bk.ap[0:64, 0:16], dtb, ALU.add, R=[bk.t, hv_t], W=[st[0]])
            ACT(beta, bk.ap[0:64, 16:32], AF.Exp, R=[bk.t], W=[st[1]], scale=-1.0)
            rel(bk)
            ACT(g_, g_, AF.Exp, R=[st[0]], W=[st[0]])
            TS("dve", beta, beta, 1.0, None, ALU.add, None, R=[st[1]], W=[st[1]])
            S.op("dve", lambda e: e.reciprocal(out=beta, in_=beta), R=[st[1]], W=[st[1]])
            yield
            ACT(g_, g_, AF.Ln, R=[st[0]], W=[st[0]], bias=1.0)
            yield
            TT("dve", g_, g_, negA[:, :], ALU.mult, R=[st[0], negA_t], W=[st[0]])
            yield
            b2 = nextbank()
            MM(b2.ap[0:64, 0:16], Utri, g_, True, True, R=[cm_t, st[0]], W=[b2.t])
            MM(b2.ap[:, 16:32], ones64, g_, True, True, R=[cm_t, st[0]], W=[b2.t])
            gb3 = g_.unsqueeze(2).broadcast_to([64, 16, 64])
            TT("dve", FX[:, :, :], gb3, Utri.unsqueeze(1).broadcast_to([64, 16, 64]), ALU.mult,
               R=[st[0], cm_t], W=[FX_t])
            yield
            ACT(gcs, b2.ap[0:64, 0:16], AF.Copy, R=[b2.t], W=[st[2]])
            ACT(egc, b2.ap[0:64, 0:16], AF.Exp, R=[b2.t], W=[st[3]])
            ACT(egl, b2.ap[:, 16:32], AF.Exp, R=[b2.t], W=[st[7]])
            ACT(FG[:, :, :], gb3, AF.Copy, R=[st[0]], W=[FG_t])
            yield
            TS("dve", negegc, egc, -1.0, None, ALU.mult, None, R=[st[3]], W=[st[4]])
            TT("dve", ekd, b2.ap[0:64, 16:32], gcs, ALU.subtract, R=[b2.t, st[2]], W=[st[5]])
            rel(b2)
            yield
            ACT(ekd, ekd, AF.Exp, R=[st[5]], W=[st[5]])
            yield
            TT("dve", bekd, ekd, beta, ALU.mult, R=[st[5], st[1]], W=[st[6]])
            for half in range(2):
                hs = slice(half * 8, half * 8 + 8)
                bA = nextbank()
                MM(bA.ap[:, :], ones64, FX[:, hs, :], True, True, R=[cm_t, FX_t], W=[bA.t])
                yield
                ACT(FE[:, hs, :], bA.ap[:, :].rearrange("p (a c) -> p a c", a=8), AF.Exp, R=[bA.t], W=[FE_t])
                rel(bA)
                yield
            yield "QK"
            TT("dve", qd[p][:, :, :].rearrange("p (a r) c -> p a r c", r=2),
               qT[:, :, tok].unsqueeze(2).broadcast_to([128, 8, 2, 64]),
               FE[:, :, :].rearrange("p (a r) c -> p a r c", r=2), ALU.mult, R=q_t + [FE_t], W=[qd_t[p]])
            yield
            for half in range(2):
                hs = slice(half * 8, half * 8 + 8)
                bD = nextbank()
                MM(bD.ap[0:64, :], ones64[:, 0:64], FX[:, hs, :], True, False, R=[cm_t, FX_t], W=[bD.t])
                MM(bD.ap[0:64, :], negU, FG[:, hs, :], False, False, R=[cm_t, FG_t], W=[bD.t])
                MM(bD.ap[0:64, :], ident_f, NEGrep, False, True, R=[cm_t], W=[bD.t])
                yield
                ACT(FD[:, hs, :], bD.ap[0:64, :].rearrange("p (a c) -> p a c", a=8), AF.Exp, R=[bD.t], W=[FD_t])
                rel(bD)
                yield
            bK = nextbank()
            bQ = nextbank()
            for hk in range(8):
                MM(bK.ap[0:64, hk * 64:(hk + 1) * 64], kT[:, hk, tok], kT[:, hk, tok], True, True,
                   R=[k_t[hk]], W=[bK.t])
                MM(bQ.ap[0:64, hk * 64:(hk + 1) * 64], kT[:, hk, tok], qT[:, hk, tok], True, True,
                   R=[k_t[hk], q_t[hk]], W=[bQ.t])
            TT("dve", FN[:, :, :], beta.unsqueeze(2).broadcast_to([64, 16, 64]),
               negstrict.unsqueeze(1).broadcast_to([64, 16, 64]), ALU.mult, R=[st[1], cm_t], W=[FN_t])
            yield
            FD4 = FD[:, :, :].rearrange("p (a r) c -> p a r c", r=2)
            TT("dve", F1[:, :, :].rearrange("p (a r) c -> p a r c", r=2),
               bK.ap[0:64, :].rearrange("p (a c) -> p a c", a=8).unsqueeze(2).broadcast_to([64, 8, 2, 64]), FD4,
               ALU.mult, R=[bK.t, FD_t], W=[F1_t])
            rel(bK)
            yield
            TT("dve", ATb[p][:, :, :].rearrange("p (a r) c -> p a r c", r=2),
               bQ.ap[0:64, :].rearrange("p (a c) -> p a c", a=8).unsqueeze(2).broadcast_to([64, 8, 2, 64]), FD4,
               ALU.mult, R=[bQ.t, FD_t], W=[AT_t[p]])
            rel(bQ)
            TT("dve", NP[:, :, 0, :], F1[:, :, :], FN[:, :, :], ALU.mult, R=[F1_t, FN_t], W=NT_t)
            yield
            for half in range(2):
                hs = slice(half * 8, half * 8 + 8)
                bT = nextbank()
                for hh in range(8):
                    h = half * 8 + hh
                    S.op("pe", lambda e: e.transpose(bT.ap[0:64, hh * 64:(hh + 1) * 64], NP[:, h, 0, :],
                                                     ident_s[0:64, 0:64]), R=[NT_t[half], ident_s_t], W=[bT.t])
                yield
                ACT(Nb[:, hs, :], bT.ap[0:64, :].rearrange("p (a c) -> p a c", a=8), AF.Copy,
                    R=[bT.t], W=[N_t[half]])
                rel(bT)
                TT("pool", NP[:, hs, 1, :], NP[:, hs, 0, :], ident_f.unsqueeze(1).broadcast_to([64, 8, 64]),
                   ALU.add, R=[NT_t[half], cm_t], W=[P_t[half]])
                yield

        def chain(c):
            p = c % 2
            tok = slice(c * 64, (c + 1) * 64)
            seg = (c * 64) // L
            sid = seqs[seg]
            smp, st = sm[p], sm_t[p]
            beta = smp[0:64, 1, :]; negegc = smp[0:64, 4, :]; bekd = smp[0:64, 6, :]; egl = smp[:, 7, :]
            so = sb_cur[0]
            sn = 1 - so
            if not prompt:
                S.dma("pool", "sld", Sst[:, :, :], sr_in[seg], W=[Sst_t])
                ACT(Sbf[so][:, :, :], Sst[:, :, :], AF.Copy, R=[Sst_t], W=[Sbf_t[so]])
            bKt = nextbank()
            for hk in range(8):
                S.op("pe", lambda e: e.transpose(bKt.bf[0:64, hk * 128:(hk + 1) * 128], kT[:, hk, tok], ident_b[:, :]),
                     R=[k_t[hk], ident_b_t], W=[bKt.t])
            ACT(Ktok[:, :, :], bKt.bf[0:64, 0:1024].rearrange("p (a c) -> p a c", a=8), AF.Copy, R=[bKt.t], W=[Ktok_t])
            rel(bKt)
            for half in range(2):
                bV = nextbank()
                for hh in range(8):
                    h = half * 8 + hh
                    S.op("pe", lambda e: e.transpose(bV.bf[0:64, hh * 128:(hh + 1) * 128], vT[:, h, tok],
                                                     ident_b[:, :]), R=[v_t[h], ident_b_t], W=[bV.t])
                ACT(Vtok[:, half * 8:half * 8 + 8, :], bV.bf[0:64, 0:1024].rearrange("p (a c) -> p a c", a=8),
                    AF.Copy, R=[bV.t], W=[Vtok_t])
                rel(bV)
            yield
            TT("pool", Sst[:, :, :], Sst[:, :, :], egl.unsqueeze(2).broadcast_to([128, 16, 128]), ALU.mult,
               R=[Sst_t, st[7]], W=[Sst_t])
            for q4 in range(4):
                h4 = slice(q4 * 4, q4 * 4 + 4)
                bS = nextbank()
                for hp in range(2):
                    hk = q4 * 2 + hp
                    MM(bS.ap[0:64, hp * 256:(hp + 1) * 256], kT[:, hk, tok], Sbf[so][:, 2 * hk:2 * hk + 2, :], True, True,
                       R=[k_t[hk], Sbf_t[so]], W=[bS.t])
                TT("dve", Wn[:, h4, :], bS.ap[0:64, :].rearrange("p (a e) -> p a e", a=4),
                   negegc[:, h4].unsqueeze(2).broadcast_to([64, 4, 128]), ALU.mult, R=[bS.t, st[4]], W=[Wn_t])
                rel(bS)
                TT("dve", Wp[:, h4, :], Wn[:, h4, :], Vtok[:, h4, :], ALU.add, R=[Wn_t, Vtok_t], W=[Wp_t])
                yield
            for q4 in range(4):
                h4 = slice(q4 * 4, q4 * 4 + 4)
                bU = nextbank()
                for hh in range(4):
                    h = q4 * 4 + hh
                    MM(bU.ap[0:64, hh * 128:(hh + 1) * 128], TTb[p][:, h, :], Wp[:, h, :], True, True,
                       R=[TTb_t[p], Wp_t], W=[bU.t])
                bU3 = bU.ap[0:64, :].rearrange("p (a e) -> p a e", a=4)
                TT("dve", vnew2[:, h4, :], bU3, bekd[:, h4].unsqueeze(2).broadcast_to([64, 4, 128]), ALU.mult,
                   R=[bU.t, st[6]], W=[vnew2_t])
                TT("dve", vnew[:, h4, :], bU3, beta[:, h4].unsqueeze(2).broadcast_to([64, 4, 128]), ALU.mult,
                   R=[bU.t, st[1]], W=[vnew_t])
                rel(bU)
                yield
            for q4 in range(4):
                h4 = slice(q4 * 4, q4 * 4 + 4)
                bS2 = nextbank()
                for hp in range(2):
                    hk = q4 * 2 + hp
                    MM(bS2.ap[:, hp * 256:(hp + 1) * 256], Ktok[:, hk, :], vnew2[:, 2 * hk:2 * hk + 2, :], True, True,
                       R=[Ktok_t, vnew2_t], W=[bS2.t])
                TT("dve", Sst[:, h4, :], bS2.ap[:, :].rearrange("p (a e) -> p a e", a=4), Sst[:, h4, :], ALU.add,
                   R=[bS2.t, Sst_t], W=[Sst_t])
                rel(bS2)
                yield
            ACT(Sbf[sn][:, :, :], Sst[:, :, :], AF.Copy, R=[Sst_t], W=[Sbf_t[sn]])
            if (not prompt) or (last and c == C - 1):
                S.dma("pool", "ost", rec_o[sid], Sst[:, :, :], R=[Sst_t], W=[])
            yield
            for half in range(2):
                hs = slice(half * 8, half * 8 + 8)
                bO = nextbank()
                for hh in range(8):
                    h = half * 8 + hh
                    MM(bO.ap[:, hh * 64:(hh + 1) * 64], Sbf[so][:, h, :], qd[p][:, h, :], True, False,
                       R=[Sbf_t[so], qd_t[p]], W=[bO.t])
                    MM(bO.ap[:, hh * 64:(hh + 1) * 64], vnew[:, h, :], ATb[p][:, h, :], False, True,
                       R=[vnew_t, AT_t[p]], W=[bO.t])
                bO3 = bO.ap[:, :].rearrange("p (a c) -> p a c", a=8)
                ACT(osq[:, hs, :], bO3, AF.Square, R=[bO.t], W=[osq_t])
                yield
                bM = nextbank()
                MM(bM.ap[:, :], ones_hd[:, :], osq[:, hs, :], True, True, R=[ones_hd_t, osq_t], W=[bM.t])
                yield
                RSQ(rso[:, hs, :], bM.ap[:, :].rearrange("p (a c) -> p a c", a=8), R=[bM.t], W=[rso_t])
                rel(bM)
                yield
                STT("dve", og[:, hs, :], bO3, vec[:, V_NORMW:V_NORMW + 1], rso[:, hs, :], ALU.mult, ALU.mult,
                    R=[bO.t, vec_t, rso_t], W=[og_t])
                rel(bO)
                yield
                TT("pool", sz[:, hs, tok], og[:, hs, :], sz[:, hs, tok], ALU.mult,
                   R=[og_t] + sz_t[half * 8:half * 8 + 8], W=sz_t[half * 8:half * 8 + 8])
                yield
            sb_cur[0] = sn

        def step(g):
            if g is None:
                return True
            try:
                next(g)
                return False
            except StopIteration:
                return True

        def run2(a, b, ratio):
            a_done = a is None
            while not a_done:
                a_done = step(a)
                for _ in range(ratio):
                    if step(b):
                        b = None
                        break
            return b

        def drain(g):
            while not step(g):
                pass

        f0 = front(0)
        lg = l2n_gen()
        lg_done = False
        while True:
            if not lg_done:
                lg_done = step(lg)
            try:
                if next(f0) == "QK":
                    break
            except StopIteration:
                break
        drain(lg)
        drain(f0)
        sv = solve(0)
        sv = run2(front(1) if C > 1 else None, sv, 2)
        drain(sv)
        for c in range(C):
            sv = solve(c + 1) if c + 1 < C else None
            sv = run2(chain(c), sv, 3)
            sv = run2(front(c + 2) if c + 2 < C else None, sv, 2)
            drain(sv)

        run_il([outp_gen(1, GW_OUT + fo, fo, T, segs) for fo in range(8)], 4)
        rms_rstd(T)
        for k in range(8):
            i = k % NPIPE
            STT("dve", tmpf[i][:, :T], xt[:, k, :T], vec[:, V_FGAIN + k:V_FGAIN + k + 1], rstd[:, :T], ALU.mult, ALU.mult,
                R=[x_t[k], vec_t, rstd_t], W=[tmpf_t[i]])
            S.dma("sp", "yst", y_dst[:, k, :], tmpf[i][:, :T], R=[tmpf_t[i]], W=[])
            if next_x is not None:
                S.dma("sp", "xld", xt[:, k, :next_x.shape[2]], next_x[:, k, :], W=[x_t[k]])

    tiles = []
    for s in range(2):
        for ti in range(n_ptiles):
            t0 = ti * TP
            tiles.append(dict(seqs=[s], n_seg=1, L=TP, x_src=xp[s][:, :, t0:t0 + TP], y_dst=yp[s][:, :, t0:t0 + TP],
                              first=ti == 0, last=ti == SEQ // TP - 1, prompt=True))
    if do_sample:
        tiles.append(dict(seqs=[2, 3, 4, 5], n_seg=4, L=64, x_src=xs, y_dst=ys, first=True, last=True, prompt=False))
    for n_, t_ in enumerate(tiles):
        nx = tiles[n_ + 1]["x_src"] if n_ + 1 < len(tiles) else None
        do_tile(x_loaded=n_ > 0, next_x=nx, **t_)
    S.barrier()


def _bf(x):
    return np.ascontiguousarray(x, dtype=np.float32)


def _consts():
    k = np.arange(64)
    cm = np.zeros((64, 1024), np.float32)
    cm[:, 0:64] = (k[:, None] <= k[None, :])
    cm[:, 64:128] = -(k[:, None] <= k[None, :]).astype(np.float32)
    cm[:, 128:192] = np.eye(64)
    cm[:, 192:256] = -(k[None, :] > k[:, None]).astype(np.float32)
    cm[:, 256:384] = 1.0
    neg = np.where(k[None, :] < k[:, None], NEGBIG, 0.0).astype(np.float32)
    cm[:, 384:896] = np.tile(neg, (1, 8))
    rc = np.zeros((128, 64), np.float32)
    for g, w in enumerate((2, 4, 8, 16)):
        rc[:, g * 16:(g + 1) * 16] = 1.0 / np.minimum(np.arange(16) + 1.0, float(w))
    return cm, rc


def _fm(v, nchunk):
    return np.asarray(v, np.float32).reshape(nchunk, 128).T


def make_in_maps(x_prompt, x_sample, c_prompt, c_sample, state_pool, state_conv, state_rec,
                 norm_gain, ada_w, ada_b, pool_w_in, pool_w_group, pool_scale, pool_w_out,
                 gdn_w_in, gdn_conv_w, gdn_a_log, gdn_dt_bias, gdn_norm_w, gdn_w_out, final_gain):
    cm, rc = _consts()
    vec = np.zeros((128, NV), np.float32)
    for l in range(2):
        vec[:, V_GAIN + l * 8:V_GAIN + (l + 1) * 8] = _fm(norm_gain[l], 8)
        vec[:, V_ADAB + l * 24:V_ADAB + (l + 1) * 24] = _fm(ada_b[l], 24)
    vec[:, V_PSCALE:V_PSCALE + 16] = _fm(pool_scale[0], 16)
    for tp in range(4):
        vec[:, V_CONVW + tp * 32:V_CONVW + (tp + 1) * 32] = _fm(gdn_conv_w[0, tp], 32)
    vec[:, V_NORMW] = np.asarray(gdn_norm_w[0], np.float32)
    vec[:, V_FGAIN:V_FGAIN + 8] = _fm(final_gain, 8)
    hv = np.zeros((64, 32), np.float32)
    hv[:, 0:16] = np.asarray(gdn_a_log[0], np.float32)[None, :]
    hv[:, 16:32] = np.asarray(gdn_dt_bias[0], np.float32)[None, :]
    shared = {
        "vec": vec, "hv": hv, "cm": cm, "rc": rc, "ident": np.eye(128, dtype=np.float32),
        "ada_w": _bf(ada_w), "pool_w_in": _bf(pool_w_in[0]), "pool_w_group": _bf(pool_w_group[0]),
        "pool_w_out": _bf(pool_w_out[0]), "gdn_w_in": _bf(gdn_w_in[0]), "gdn_w_out": _bf(gdn_w_out[0]),
    }
    x_prompt = np.asarray(x_prompt, np.float32)
    x_sample = np.asarray(x_sample, np.float32)
    in_maps = []
    for i in range(NCORES):
        xp = x_prompt[2 * i:2 * i + 2].reshape(2, SEQ, 8, 128).transpose(0, 3, 2, 1)
        xs = x_sample[4 * i:4 * i + 4].reshape(4, 64, 8, 128).transpose(3, 2, 0, 1).reshape(128, 8, 256)
        c6 = np.concatenate([np.asarray(c_prompt)[2 * i:2 * i + 2], np.asarray(c_sample)[4 * i:4 * i + 4]], 0)
        cT = c6.reshape(6, 8, 128).transpose(2, 1, 0)
        sp = np.asarray(state_pool)[0, 4 * i:4 * i + 4].reshape(4, 15, 16, 128).transpose(3, 2, 0, 1)
        sc = np.asarray(state_conv)[0, 4 * i:4 * i + 4].reshape(4, 3, 32, 128).transpose(3, 2, 0, 1)
        sr = np.asarray(state_rec)[0, 4 * i:4 * i + 4].transpose(0, 2, 1, 3)
        m = dict(shared)
        m.update({"xp": _bf(xp), "xs": _bf(xs), "cT": _bf(cT), "sp_in": _bf(sp), "sc_in": _bf(sc), "sr_in": _bf(sr)})
        in_maps.append(m)
    return in_maps


def assemble(results):
    y_prompt = np.zeros((16, SEQ, D), np.float32)
    y_sample = np.zeros((32, 64, D), np.float32)
    pool_p = np.zeros((1, 16, 15, 2048), np.float32)
    conv_p = np.zeros((1, 16, 3, 4096), np.float32)
    rec_p = np.zeros((1, 16, 16, 128, 128), np.float32)
    pool_s = np.zeros((1, 32, 15, 2048), np.float32)
    conv_s = np.zeros((1, 32, 3, 4096), np.float32)
    rec_s = np.zeros((1, 32, 16, 128, 128), np.float32)
    for i, r in enumerate(results):
        yp = np.asarray(r["yp"])
        y_prompt[2 * i:2 * i + 2] = yp.transpose(0, 3, 2, 1).reshape(2, SEQ, D)
        ys = np.asarray(r["ys"]).reshape(128, 8, 4, 64)
        y_sample[4 * i:4 * i + 4] = ys.transpose(2, 3, 1, 0).reshape(4, 64, D)
        po = np.asarray(r["pool_o"]).transpose(0, 3, 2, 1).reshape(6, 15, 2048)
        co = np.asarray(r["conv_o"]).transpose(0, 3, 2, 1).reshape(6, 3, 4096)
        ro = np.asarray(r["rec_o"]).transpose(0, 2, 1, 3)
        pool_p[0, 2 * i:2 * i + 2] = po[0:2]
        pool_s[0, 4 * i:4 * i + 4] = po[2:6]
        conv_p[0, 2 * i:2 * i + 2] = co[0:2]
        conv_s[0, 4 * i:4 * i + 4] = co[2:6]
        rec_p[0, 2 * i:2 * i + 2] = ro[0:2]
        rec_s[0, 4 * i:4 * i + 4] = ro[2:6]
    return (y_prompt, y_sample, pool_p, conv_p, rec_p, pool_s, conv_s, rec_s)


_NC_CACHE = {}


def kernel(**inputs):
    in_maps = make_in_maps(**inputs)
    if "nc" not in _NC_CACHE:
        _NC_CACHE["nc"] = build()
    res = run_bass_kernel_spmd(_NC_CACHE["nc"], in_maps, core_ids=list(range(NCORES)))
    return assemble(res.results)
```
